# Optimizing a Trainium2 kernel written in Bass

```python
import math
import jax, jax.numpy as jnp
from jax import lax
import numpy as np

D_MODEL = 1024
BATCH = 4
SEQ = 4096
DEPTH = 2

DN_ALPHA = (2.0 * DEPTH) ** 0.25
DN_BETA = (8.0 * DEPTH) ** -0.25
LN_EPS = 1e-5
NEG_INF = -1e30

D_FF = 256 * ((8 * D_MODEL // 3 + 255) // 256)
MACARON_WEIGHT = 0.5

CONV_CH = D_MODEL
CONV_WIDTH = 31

SSM_D_INNER = D_MODEL
SSM_HEAD_DIM = 64
SSM_HEADS = SSM_D_INNER // SSM_HEAD_DIM
SSM_GROUPS = 4
SSM_HEADS_PER_GROUP = SSM_HEADS // SSM_GROUPS
SSM_STATE = 128
SSM_CONV_WIDTH = 4
SSM_CHUNK = 256
SSM_CONV_DIM = SSM_D_INNER + 2 * SSM_GROUPS * SSM_STATE

NSA_HEAD_DIM = 64
NSA_HEADS = D_MODEL // NSA_HEAD_DIM
NSA_KV_HEADS = 4
NSA_GQA = NSA_HEADS // NSA_KV_HEADS
CMP_LEN = 32
CMP_STRIDE = 16
CMP_HIDDEN = 2 * NSA_HEAD_DIM
SEL_BLOCK = 64
N_SELECT = 16
WINDOW = 512
Q_BLOCK = 64
FORCED_SCORE = 1e4
ROPE_THETA = 10000.0

N_BRANCH = 3
A_IN = 2 * CONV_CH
B_IN = 2 * SSM_D_INNER + 2 * SSM_GROUPS * SSM_STATE + SSM_HEADS
C_IN = NSA_HEADS * NSA_HEAD_DIM + 6 * NSA_KV_HEADS * NSA_HEAD_DIM + 3 * NSA_HEADS
GATE_IN = N_BRANCH * D_MODEL
IN_COLS = A_IN + B_IN + C_IN + GATE_IN

kernel_name = 'hybrid_conv_ssd_nsa_macaron_deepnorm'


def _split(x, sizes):
    return jnp.split(x, [int(s) for s in np.cumsum(sizes)[:-1]], axis=-1)


def layer_norm(x, g, b):
    x32 = x.astype(jnp.float32)
    mu = jnp.mean(x32, -1, keepdims=True)
    var = jnp.mean(jnp.square(x32 - mu), -1, keepdims=True)
    return ((x32 - mu) * lax.rsqrt(var + LN_EPS) * g + b).astype(x.dtype)


def rms_norm(x, g):
    x32 = x.astype(jnp.float32)
    return x32 * lax.rsqrt(jnp.mean(jnp.square(x32), -1, keepdims=True) + LN_EPS) * g


def causal_dwconv(x, w, b):
    width, ch = w.shape
    y = lax.conv_general_dilated(x, w[:, None, :].astype(x.dtype), (1,), [(width - 1, 0)],
                                 dimension_numbers=('NWC', 'WIO', 'NWC'), feature_group_count=ch)
    return y + b


def rope(x, pos):
    half = x.shape[-1] // 2
    inv_freq = ROPE_THETA ** (-jnp.arange(half, dtype=jnp.float32) / half)
    ang = pos.astype(jnp.float32)[:, None] * inv_freq[None, :]
    cos = jnp.cos(ang)[:, None, :]
    sin = jnp.sin(ang)[:, None, :]
    x32 = x.astype(jnp.float32)
    x1, x2 = x32[..., :half], x32[..., half:]
    return jnp.concatenate([x1 * cos - x2 * sin, x2 * cos + x1 * sin], -1).astype(x.dtype)


def modulate(h, shift, scale):
    return h * (1.0 + scale) + shift


def swiglu_ffn(u, w_gate, w_up, w_down):
    return (jax.nn.silu(u @ w_gate) * (u @ w_up)) @ w_down


def conformer_conv(a_in, conv_w, conv_b, norm_g, norm_b):
    val, gate = _split(a_in, [CONV_CH, CONV_CH])
    a = val * jax.nn.sigmoid(gate)
    a = causal_dwconv(a, conv_w, conv_b)
    return jax.nn.silu(layer_norm(a, norm_g, norm_b))


def ssd_chunked(x, a, b, c):
    bsz, seq, ng, nr, hp = x.shape
    chunk = math.gcd(SSM_CHUNK, seq)
    nc = seq // chunk
    xc = x.reshape(bsz, nc, chunk, ng, nr, hp)
    bc = b.reshape(bsz, nc, chunk, ng, -1)
    cc = c.reshape(bsz, nc, chunk, ng, -1)
    a_cs = jnp.cumsum(a.reshape(bsz, nc, chunk, ng, nr).transpose(0, 3, 4, 1, 2), axis=-1)
    causal = jnp.tril(jnp.ones((chunk, chunk), bool))
    seg = jnp.exp(jnp.where(causal, a_cs[..., :, None] - a_cs[..., None, :], NEG_INF))
    cb = jnp.einsum('bclgn,bcsgn->bgcls', cc, bc)
    y_diag = jnp.einsum('bgrcls,bcsgrp->bclgrp', cb[:, :, None] * seg, xc)
    decay_in = jnp.exp(a_cs[..., -1:] - a_cs)
    chunk_states = jnp.einsum('bclgn,bgrcl,bclgrp->cbgrpn', bc, decay_in, xc)
    chunk_decay = jnp.exp(a_cs[..., -1]).transpose(3, 0, 1, 2)

    def step(h, inp):
        st, dec = inp
        return h * dec[..., None, None] + st, h

    h0 = jnp.zeros(chunk_states.shape[1:], chunk_states.dtype)
    _, h_prev = lax.scan(step, h0, (chunk_states, chunk_decay))
    y_off = jnp.einsum('bclgn,cbgrpn,bgrcl->bclgrp', cc, h_prev, jnp.exp(a_cs))
    return (y_diag + y_off).reshape(bsz, seq, ng, nr, hp)


def mamba2_ssd(ssm_in, conv_w, conv_b, dt_bias, a_log, d_skip, norm_w):
    bsz, seq, _ = ssm_in.shape
    f32 = jnp.float32
    gn = SSM_GROUPS * SSM_STATE
    z, xbc, dt = _split(ssm_in, [SSM_D_INNER, SSM_CONV_DIM, SSM_HEADS])
    xbc = jax.nn.silu(causal_dwconv(xbc, conv_w, conv_b))
    xs, b_in, c_in = _split(xbc, [SSM_D_INNER, gn, gn])
    dt = jax.nn.softplus(dt.astype(f32) + dt_bias.astype(f32)).reshape(bsz, seq, SSM_GROUPS, SSM_HEADS_PER_GROUP)
    a = -jnp.exp(a_log.astype(f32)).reshape(SSM_GROUPS, SSM_HEADS_PER_GROUP)
    xh = xs.astype(f32).reshape(bsz, seq, SSM_GROUPS, SSM_HEADS_PER_GROUP, SSM_HEAD_DIM)
    bm = b_in.astype(f32).reshape(bsz, seq, SSM_GROUPS, SSM_STATE)
    cm = c_in.astype(f32).reshape(bsz, seq, SSM_GROUPS, SSM_STATE)
    y = ssd_chunked(xh * dt[..., None], a * dt, bm, cm)
    y = y + d_skip.astype(f32).reshape(SSM_GROUPS, SSM_HEADS_PER_GROUP, 1) * xh
    y = y.reshape(bsz, seq, SSM_D_INNER) * jax.nn.silu(z.astype(f32))
    return rms_norm(y, norm_w).astype(ssm_in.dtype)


def compress_blocks(kv, pe, w1, w2):
    bsz, seq, nkv, hd = kv.shape
    n_cmp = (seq - CMP_LEN) // CMP_STRIDE + 1
    idx = jnp.arange(n_cmp)[:, None] * CMP_STRIDE + jnp.arange(CMP_LEN)[None, :]
    blocks = kv[:, idx] + pe[:, None, :]
    flat = blocks.transpose(0, 1, 3, 2, 4).reshape(bsz, n_cmp, nkv, CMP_LEN * hd)
    return jax.nn.gelu(flat @ w1) @ w2


def nsa_attention(nsa_in, pe_k, pe_v, k_w1, k_w2, v_w1, v_w2):
    bsz, seq, _ = nsa_in.shape
    dt_ = nsa_in.dtype
    f32 = jnp.float32
    kvd = NSA_KV_HEADS * NSA_HEAD_DIM
    q, k_c, v_c, k_s, v_s, k_w, v_w, gate_logits = _split(nsa_in, [NSA_HEADS * NSA_HEAD_DIM] + [kvd] * 6 + [3 * NSA_HEADS])
    pos = jnp.arange(seq)
    kv_shape = (bsz, seq, NSA_KV_HEADS, NSA_HEAD_DIM)
    q = rope(q.reshape(bsz, seq, NSA_HEADS, NSA_HEAD_DIM), pos).reshape(bsz, seq, NSA_KV_HEADS, NSA_GQA, NSA_HEAD_DIM)
    n_cmp = (seq - CMP_LEN) // CMP_STRIDE + 1
    cmp_end = jnp.arange(n_cmp) * CMP_STRIDE + CMP_LEN - 1
    k_cmp = rope(compress_blocks(k_c.reshape(kv_shape), pe_k, k_w1, k_w2), cmp_end)
    v_cmp = compress_blocks(v_c.reshape(kv_shape), pe_v, v_w1, v_w2)
    n_sb = seq // SEL_BLOCK
    n_sel = min(N_SELECT, n_sb)
    blk_shape = (bsz, n_sb, SEL_BLOCK, NSA_KV_HEADS, NSA_HEAD_DIM)
    k_sel = rope(k_s.reshape(kv_shape), pos).reshape(blk_shape).transpose(0, 3, 1, 2, 4)
    v_sel = v_s.reshape(blk_shape).transpose(0, 3, 1, 2, 4)
    pad = ((0, 0), (WINDOW, 0), (0, 0), (0, 0))
    k_win = jnp.pad(rope(k_w.reshape(kv_shape), pos), pad)
    v_win = jnp.pad(v_w.reshape(kv_shape), pad)
    gates = jax.nn.sigmoid(gate_logits.astype(f32)).astype(dt_).reshape(bsz, seq, NSA_KV_HEADS, NSA_GQA, 3)
    c_start = jnp.arange(n_cmp) * CMP_STRIDE
    s_start = jnp.arange(n_sb) * SEL_BLOCK
    overlap = ((c_start[:, None] < s_start[None, :] + SEL_BLOCK) &
               (c_start[:, None] + CMP_LEN > s_start[None, :])).astype(f32)
    scale = NSA_HEAD_DIM ** -0.5
    b_idx = jnp.arange(bsz)[:, None, None, None]
    g_idx = jnp.arange(NSA_KV_HEADS)[None, :, None, None]
    blk_ids = jnp.arange(n_sb)

    def query_block(i):
        q0 = i * Q_BLOCK
        t = q0 + jnp.arange(Q_BLOCK)
        qb = lax.dynamic_slice_in_dim(q, q0, Q_BLOCK, axis=1)
        s = jnp.einsum('bqgrd,bngd->bgrqn', qb, k_cmp).astype(f32) * scale
        cmask = cmp_end[None, :] <= t[:, None]
        p_cmp = jax.nn.softmax(jnp.where(cmask, s, NEG_INF), -1) * cmask
        o_cmp = jnp.einsum('bgrqn,bngd->bqgrd', p_cmp.astype(dt_), v_cmp)
        imp = jnp.einsum('bgrqn,nj->bgqj', p_cmp, overlap)
        cur = (t // SEL_BLOCK)[:, None]
        forced = (blk_ids[None, :] == 0) | (blk_ids[None, :] == cur) | (blk_ids[None, :] == cur - 1)
        future = blk_ids[None, :] * SEL_BLOCK > t[:, None]
        imp = jnp.where(forced, FORCED_SCORE, jnp.where(future, -1.0, imp))
        _, sel = lax.top_k(imp, n_sel)
        kg = k_sel[b_idx, g_idx, sel]
        vg = v_sel[b_idx, g_idx, sel]
        s = jnp.einsum('bqgrd,bgqkld->bgrqkl', qb, kg).astype(f32) * scale
        kpos = sel[..., None] * SEL_BLOCK + jnp.arange(SEL_BLOCK)
        smask = (kpos <= t[None, None, :, None, None])[:, :, None]
        s = jnp.where(smask, s, NEG_INF).reshape(bsz, NSA_KV_HEADS, NSA_GQA, Q_BLOCK, n_sel * SEL_BLOCK)
        p_sel = jax.nn.softmax(s, -1).reshape(bsz, NSA_KV_HEADS, NSA_GQA, Q_BLOCK, n_sel, SEL_BLOCK)
        o_sel = jnp.einsum('bgrqkl,bgqkld->bqgrd', p_sel.astype(dt_), vg)
        kw = lax.dynamic_slice_in_dim(k_win, q0, WINDOW + Q_BLOCK, axis=1)
        vw = lax.dynamic_slice_in_dim(v_win, q0, WINDOW + Q_BLOCK, axis=1)
        wpos = q0 - WINDOW + jnp.arange(WINDOW + Q_BLOCK)
        wmask = (wpos[None, :] <= t[:, None]) & (wpos[None, :] > t[:, None] - WINDOW) & (wpos[None, :] >= 0)
        s = jnp.einsum('bqgrd,bkgd->bgrqk', qb, kw).astype(f32) * scale
        p_win = jax.nn.softmax(jnp.where(wmask, s, NEG_INF), -1)
        o_win = jnp.einsum('bgrqk,bkgd->bqgrd', p_win.astype(dt_), vw)
        g = lax.dynamic_slice_in_dim(gates, q0, Q_BLOCK, axis=1)
        return g[..., 0:1] * o_cmp + g[..., 1:2] * o_sel + g[..., 2:3] * o_win

    out = lax.map(query_block, jnp.arange(seq // Q_BLOCK))
    return out.transpose(1, 0, 2, 3, 4, 5).reshape(bsz, seq, NSA_HEADS * NSA_HEAD_DIM)


def hybrid_token_mixer(u, w_in, conv_a_w, conv_a_b, norm_a_g, norm_a_b, w_a_out,
                       ssm_conv_w, ssm_conv_b, ssm_dt_bias, ssm_a_log, ssm_d, ssm_norm_w, w_b_out,
                       cmp_pe_k, cmp_pe_v, cmp_k_w1, cmp_k_w2, cmp_v_w1, cmp_v_w2, w_c_out, w_o):
    bsz, seq, _ = u.shape
    proj = jnp.einsum('bsd,de->bse', u, w_in)
    a_in, b_in, c_in, g_in = _split(proj, [A_IN, B_IN, C_IN, GATE_IN])
    y_a = conformer_conv(a_in, conv_a_w, conv_a_b, norm_a_g, norm_a_b) @ w_a_out
    y_b = mamba2_ssd(b_in, ssm_conv_w, ssm_conv_b, ssm_dt_bias, ssm_a_log, ssm_d, ssm_norm_w) @ w_b_out
    y_c = nsa_attention(c_in, cmp_pe_k, cmp_pe_v, cmp_k_w1, cmp_k_w2, cmp_v_w1, cmp_v_w2) @ w_c_out
    g = jax.nn.sigmoid(g_in.astype(jnp.float32)).astype(u.dtype).reshape(bsz, seq, N_BRANCH, D_MODEL)
    merged = g[:, :, 0] * y_a + g[:, :, 1] * y_b + g[:, :, 2] * y_c
    return merged @ w_o


def setup_inputs(seed: int = 0) -> dict:
    key = jax.random.key(seed)
    keys = iter(jax.random.split(key, 48))
    f32 = jnp.float32
    L = DEPTH

    def normal(shape, scale):
        return jax.random.normal(next(keys), shape, f32) * scale

    def uniform(shape, lo, hi):
        return jax.random.uniform(next(keys), shape, f32, lo, hi)

    kvd = NSA_KV_HEADS * NSA_HEAD_DIM
    col_scale = np.concatenate([
        np.ones(A_IN + B_IN + NSA_HEADS * NSA_HEAD_DIM),
        np.tile(np.concatenate([np.ones(kvd), np.full(kvd, DN_BETA)]), 3),
        np.ones(3 * NSA_HEADS + GATE_IN)]).astype(np.float32)
    dt0 = jnp.exp(uniform((L, SSM_HEADS), math.log(1e-3), math.log(1e-1)))
    cmp_in = CMP_LEN * NSA_HEAD_DIM
    return {
        'x': normal((BATCH, SEQ, D_MODEL), 1.0),
        'c': normal((BATCH, D_MODEL), 1.0),
        'ada_w': normal((L, D_MODEL, 9 * D_MODEL), 0.5 * D_MODEL ** -0.5),
        'ada_b': normal((L, 9 * D_MODEL), 0.01),
        'ln_g': 1.0 + normal((L, 3, D_MODEL), 0.02),
        'ln_b': normal((L, 3, D_MODEL), 0.02),
        'ffn_w_gate': normal((L, 2, D_MODEL, D_FF), D_MODEL ** -0.5),
        'ffn_w_up': normal((L, 2, D_MODEL, D_FF), D_MODEL ** -0.5),
        'ffn_w_down': normal((L, 2, D_FF, D_MODEL), DN_BETA * D_FF ** -0.5),
        'w_in': normal((L, D_MODEL, IN_COLS), D_MODEL ** -0.5) * jnp.asarray(col_scale),
        'conv_a_w': normal((L, CONV_WIDTH, CONV_CH), CONV_WIDTH ** -0.5),
        'conv_a_b': normal((L, CONV_CH), 0.01),
        'norm_a_g': 1.0 + normal((L, CONV_CH), 0.02),
        'norm_a_b': normal((L, CONV_CH), 0.02),
        'w_a_out': normal((L, CONV_CH, D_MODEL), CONV_CH ** -0.5),
        'ssm_conv_w': normal((L, SSM_CONV_WIDTH, SSM_CONV_DIM), SSM_CONV_WIDTH ** -0.5),
        'ssm_conv_b': normal((L, SSM_CONV_DIM), 0.01),
        'ssm_dt_bias': dt0 + jnp.log(-jnp.expm1(-dt0)),
        'ssm_a_log': jnp.log(uniform((L, SSM_HEADS), 1.0, 16.0)),
        'ssm_d': 1.0 + normal((L, SSM_HEADS), 0.1),
        'ssm_norm_w': 1.0 + normal((L, SSM_D_INNER), 0.02),
        'w_b_out': normal((L, SSM_D_INNER, D_MODEL), SSM_D_INNER ** -0.5),
        'cmp_pe_k': normal((L, CMP_LEN, NSA_HEAD_DIM), 0.02),
        'cmp_pe_v': normal((L, CMP_LEN, NSA_HEAD_DIM), 0.02),
        'cmp_k_w1': normal((L, cmp_in, CMP_HIDDEN), cmp_in ** -0.5),
        'cmp_k_w2': normal((L, CMP_HIDDEN, NSA_HEAD_DIM), CMP_HIDDEN ** -0.5),
        'cmp_v_w1': normal((L, cmp_in, CMP_HIDDEN), cmp_in ** -0.5),
        'cmp_v_w2': normal((L, CMP_HIDDEN, NSA_HEAD_DIM), CMP_HIDDEN ** -0.5),
        'w_c_out': normal((L, NSA_HEADS * NSA_HEAD_DIM, D_MODEL), (NSA_HEADS * NSA_HEAD_DIM) ** -0.5),
        'w_o': normal((L, D_MODEL, D_MODEL), DN_BETA * D_MODEL ** -0.5),
    }


def reference(x, c, ada_w, ada_b, ln_g, ln_b, ffn_w_gate, ffn_w_up, ffn_w_down, w_in,
              conv_a_w, conv_a_b, norm_a_g, norm_a_b, w_a_out,
              ssm_conv_w, ssm_conv_b, ssm_dt_bias, ssm_a_log, ssm_d, ssm_norm_w, w_b_out,
              cmp_pe_k, cmp_pe_v, cmp_k_w1, cmp_k_w2, cmp_v_w1, cmp_v_w2, w_c_out, w_o):
    bsz = c.shape[0]
    h = x
    for l in range(DEPTH):
        mod = (jax.nn.silu(c) @ ada_w[l] + ada_b[l]).reshape(bsz, 3, 3, 1, D_MODEL)
        u = modulate(h, mod[:, 0, 0], mod[:, 0, 1])
        y = swiglu_ffn(u, ffn_w_gate[l, 0], ffn_w_up[l, 0], ffn_w_down[l, 0])
        h = layer_norm(DN_ALPHA * h + MACARON_WEIGHT * mod[:, 0, 2] * y, ln_g[l, 0], ln_b[l, 0])
        u = modulate(h, mod[:, 1, 0], mod[:, 1, 1])
        y = hybrid_token_mixer(u, w_in[l], conv_a_w[l], conv_a_b[l], norm_a_g[l], norm_a_b[l], w_a_out[l],
                               ssm_conv_w[l], ssm_conv_b[l], ssm_dt_bias[l], ssm_a_log[l], ssm_d[l], ssm_norm_w[l], w_b_out[l],
                               cmp_pe_k[l], cmp_pe_v[l], cmp_k_w1[l], cmp_k_w2[l], cmp_v_w1[l], cmp_v_w2[l], w_c_out[l], w_o[l])
        h = layer_norm(DN_ALPHA * h + mod[:, 1, 2] * y, ln_g[l, 1], ln_b[l, 1])
        u = modulate(h, mod[:, 2, 0], mod[:, 2, 1])
        y = swiglu_ffn(u, ffn_w_gate[l, 1], ffn_w_up[l, 1], ffn_w_down[l, 1])
        h = layer_norm(DN_ALPHA * h + MACARON_WEIGHT * mod[:, 2, 2] * y, ln_g[l, 2], ln_b[l, 2])
    return h
```

```python
import math
import numpy as np
import concourse.bass as bass
import concourse.mybir as mybir
from concourse.bass_utils import run_bass_kernel_spmd
from contextlib import ExitStack

F32 = mybir.dt.float32
BF16 = mybir.dt.bfloat16
AF = mybir.ActivationFunctionType
ALU = mybir.AluOpType
AX = mybir.AxisListType

EPOCH = 30000
SSDVAR = 2

D = 1024
S = 4096
DFF = 2816
L = 2
DN_ALPHA = (2.0 * L) ** 0.25
LN_EPS = 1e-5
NCORES = 4


class Buf:
    __slots__ = ("name", "w", "r", "sem", "dval")

    def __init__(self, name):
        self.name = name
        self.w = None
        self.r = {}
        self.sem = None
        self.dval = 0


class Prog:
    ENG = ["pe", "dve", "act", "pool", "sp"]

    def __init__(self, nc):
        self.nc = nc
        self.streams = {e: [] for e in self.ENG}
        self.cnt = {e: 0 for e in self.ENG}
        self.known = {e: {} for e in self.ENG}
        self.semkeys = {}
        self.pending = {e: {} for e in self.ENG}
        self.dma_bufs = []
        self.free_sems = []
        self.nsem = 0

    def _deps(self, reads, writes):
        deps = {}
        for b in reads:
            t = b.w
            if t is not None and deps.get(t[0], 0) < t[1]:
                deps[t[0]] = t[1]
        for b in writes:
            t = b.w
            if t is not None and deps.get(t[0], 0) < t[1]:
                deps[t[0]] = t[1]
            for k, v in b.r.items():
                if deps.get(k, 0) < v:
                    deps[k] = v
        return deps

    def _commit(self, tok, reads, writes):
        k, v = tok
        for b in reads:
            if b.r.get(k, 0) < v:
                b.r[k] = v
        for b in writes:
            b.w = tok
            b.r = {}

    def _filter(self, eng, deps):
        pend = self.pending[eng]
        if pend:
            for k, v in pend.items():
                if deps.get(k, 0) < v:
                    deps[k] = v
            self.pending[eng] = {}
        waits = []
        kn = self.known[eng]
        for k, v in deps.items():
            if eng == "pe" and k.startswith("Epe"):
                continue
            if kn.get(k, 0) >= v:
                continue
            kn[k] = v
            waits.append((k, v))
            self.semkeys[k] = True
        return waits

    def op(self, eng, fn, reads=(), writes=()):
        deps = self._deps(reads, writes)
        waits = self._filter(eng, deps)
        c = self.cnt[eng]
        self.cnt[eng] = c + 1
        key = "E%s%d" % (eng, c // EPOCH)
        tok = (key, c % EPOCH + 1)
        self.semkeys[key] = True
        self.streams[eng].append((waits, fn, key, 1))
        self._commit(tok, reads, writes)
        return tok

    def dma(self, out, in_, sb, reads=(), writes=(), q="sp"):
        deps = self._deps(reads, writes)
        waits = self._filter(q, deps)
        if sb.sem is not None and sb.dval + 16 > EPOCH:
            sb.sem = None
        if sb.sem is None:
            while self.free_sems:
                nm, val = self.free_sems.pop()
                if val + 4096 <= EPOCH:
                    sb.sem, sb.dval = nm, val
                    break
            if sb.sem is None:
                self.nsem += 1
                sb.sem, sb.dval = "D%d" % self.nsem, 0
            self.dma_bufs.append(sb)
        sb.dval += 16
        key = sb.sem
        tok = (key, sb.dval)
        self.semkeys[key] = True
        self.streams[q].append((waits, lambda e: e.dma_start(out=out, in_=in_), key, 16))
        self._commit(tok, reads, writes)
        return tok

    def barrier(self):
        allk = {}
        for e in self.ENG:
            c = self.cnt[e]
            if c:
                allk["E%s%d" % (e, (c - 1) // EPOCH)] = (c - 1) % EPOCH + 1
        for b in self.dma_bufs:
            if b.sem is not None:
                if allk.get(b.sem, 0) < b.dval:
                    allk[b.sem] = b.dval
                self.free_sems.append((b.sem, b.dval))
                b.sem = None
        self.dma_bufs = []
        for e in self.ENG:
            p = self.pending[e]
            for k, v in allk.items():
                if p.get(k, 0) < v:
                    p[k] = v

    def dbuf(self, name):
        return Buf(name)

    def emit(self):
        nc = self.nc
        self.barrier()
        for e in self.ENG:
            waits = self._filter(e, {})
            self.streams[e].append((waits, None, None, 0))
        with ExitStack() as st:
            sems = {}
            for k in self.semkeys:
                sems[k] = st.enter_context(nc.semaphore(k))
            block = st.enter_context(nc.Block())

            def run(e, stream):
                for (waits, fn, key, inc) in stream:
                    for (k, v) in waits:
                        e.wait_ge(sems[k], v)
                    if fn is not None:
                        fn(e).then_inc(sems[key], inc)

            @block.tensor
            def _(e):
                run(e, self.streams["pe"])

            @block.vector
            def _(e):
                run(e, self.streams["dve"])

            @block.scalar
            def _(e):
                run(e, self.streams["act"])

            @block.gpsimd
            def _(e):
                run(e, self.streams["pool"])

            @block.sync
            def _(e):
                run(e, self.streams["sp"])


class Arena:
    def __init__(self, nc):
        self.nc = nc
        self.off = 16512
        self.limit = 229376
        self.n = 0

    def alloc(self, shape, dtype, name="t"):
        nbytes = int(np.prod(shape[1:])) * (4 if dtype == F32 else 2)
        nbytes = (nbytes + 63) // 64 * 64
        self.n += 1
        t = self.nc.alloc_sbuf_tensor_at("%s_%d" % (name, self.n), list(shape), dtype, offset=self.off)
        self.off += nbytes
        assert self.off <= self.limit, "SBUF arena overflow %d" % self.off
        return t

    def mark(self):
        return self.off

    def release(self, m):
        self.off = m


class KB:
    def __init__(self, nc):
        self.nc = nc
        self.P = Prog(nc)
        self.A = Arena(nc)
        self.ps = [nc.alloc_psum_tensor("ps%d" % i, [128, 512], F32) for i in range(8)]
        self.psb = [Buf("ps%d" % i) for i in range(8)]
        self.psi = 0
        self.nrot = 8
        self.din = {}
        self.nb = 0

    def inp(self, name, shape, dtype=F32):
        t = self.nc.dram_tensor(name, list(shape), dtype, kind="ExternalInput")
        self.din[name] = t
        return t

    def scratch(self, name, shape, dtype=F32):
        return self.nc.dram_tensor(name, list(shape), dtype, kind="Internal")

    def buf(self, name="b"):
        self.nb += 1
        return Buf("%s%d" % (name, self.nb))

    def dbuf(self, name="d"):
        self.nb += 1
        return self.P.dbuf("%s%d" % (name, self.nb))

    def sb(self, shape, dtype=F32, name="t", dma=False):
        t = self.A.alloc(shape, dtype, name)
        b = self.dbuf(name) if dma else self.buf(name)
        return t, b

    def psum(self):
        i = self.psi % self.nrot
        self.psi = (i + 1) % self.nrot
        return self.ps[i], self.psb[i]

    def mm(self, out, lhsT, rhs, st, sp, R, W):
        self.P.op("pe", lambda e: e.matmul(out, lhsT=lhsT, rhs=rhs, start=st, stop=sp), R, W)

    def tr(self, out, in_, ident, R, W):
        self.P.op("pe", lambda e: e.transpose(out, in_, ident), R, W)

    def act(self, out, in_, func, R, W, bias=0.0, scale=1.0):
        self.P.op("act", lambda e: e.activation(out=out, in_=in_, func=func, bias=bias, scale=scale), R, W)

    def tt(self, eng, out, in0, in1, op, R, W):
        self.P.op(eng, lambda e: e.tensor_tensor(out=out, in0=in0, in1=in1, op=op), R, W)

    def ts(self, eng, out, in0, s1, s2, op0, op1, R, W):
        if s2 is None:
            self.P.op(eng, lambda e: e.tensor_scalar(out=out, in0=in0, scalar1=s1, scalar2=None, op0=op0), R, W)
        else:
            self.P.op(eng, lambda e: e.tensor_scalar(out=out, in0=in0, scalar1=s1, scalar2=s2, op0=op0, op1=op1), R, W)

    def stt(self, out, in0, scalar, in1, op0, op1, R, W):
        self.P.op("dve", lambda e: e.scalar_tensor_tensor(out=out, in0=in0, scalar=scalar, in1=in1, op0=op0, op1=op1), R, W)

    def cp(self, eng, out, in_, R, W):
        if eng == "act":
            self.P.op("act", lambda e: e.copy(out=out, in_=in_), R, W)
        else:
            self.P.op(eng, lambda e: e.tensor_copy(out=out, in_=in_), R, W)

    def memset(self, eng, ap, val, W):
        self.P.op(eng, lambda e: e.memset(ap, val), (), W)

    def recip(self, out, in_, R, W):
        self.P.op("dve", lambda e: e.reciprocal(out=out, in_=in_), R, W)

    def dma(self, out, in_, sb, R=(), W=()):
        self.P.dma(out, in_, sb, R, W)


class WStream:
    def __init__(self, kb, n, nslots=2, name="w"):
        self.kb = kb
        self.n = n
        self.st = [kb.sb([128, n], F32, name + "s", dma=True) for _ in range(nslots)]
        self.bf = [kb.sb([128, n], BF16, name + "b") for _ in range(nslots)]
        self.i = 0
        self.ns = nslots
        self.q = []

    def prefetch(self, src, n=None):
        n = n or self.n
        kb = self.kb
        i = self.i
        self.i = (i + 1) % self.ns
        st, stb = self.st[i]
        bf, bfb = self.bf[i]
        kb.dma(st[:, 0:n], src, stb, (), [stb])
        self.q.append((i, n))

    def get(self, eng="pool"):
        kb = self.kb
        i, n = self.q.pop(0)
        st, stb = self.st[i]
        bf, bfb = self.bf[i]
        kb.cp(eng, bf[:, 0:n], st[:, 0:n], [stb], [bfb])
        return bf, bfb


def colnorm_stats(kb, C, src_fn, nk, N, ones, onesb, sq, sqb, eps, center=True):
    ps1, ps1b = kb.psum()
    ps2, ps2b = kb.psum()
    for k in range(nk):
        x, xb = src_fn(k)
        if center:
            kb.mm(ps1[:, 0:N], ones[:], x, k == 0, k == nk - 1, [onesb, xb], [ps1b])
        kb.act(sq[:, 0:N], x, AF.Square, [xb], [sqb])
        kb.mm(ps2[:, 0:N], ones[:], sq[:, 0:N], k == 0, k == nk - 1, [onesb, sqb], [ps2b])
    mean, meanb = C["mean"]
    rstd, rstdb = C["rstd"]
    inv = 1.0 / (128 * nk)
    if center:
        kb.act(mean[:, 0:N], ps1[:, 0:N], AF.Identity, [ps1b], [meanb], scale=inv)
        kb.tt("dve", rstd[:, 0:N], mean[:, 0:N], mean[:, 0:N], ALU.mult, [meanb], [rstdb])
        kb.stt(rstd[:, 0:N], ps2[:, 0:N], inv, rstd[:, 0:N], ALU.mult, ALU.subtract, [ps2b, rstdb], [rstdb])
        kb.act(rstd[:, 0:N], rstd[:, 0:N], AF.Sqrt, [rstdb], [rstdb], bias=C["epscol"][0][:, 0:1])
    else:
        kb.act(rstd[:, 0:N], ps2[:, 0:N], AF.Sqrt, [ps2b], [rstdb], bias=C["epscol"][0][:, 0:1], scale=inv)
    kb.recip(rstd[:, 0:N], rstd[:, 0:N], [rstdb], [rstdb])


def host_consts():
    c = {}
    half = 32
    inv_freq = (10000.0 ** (-np.arange(half, dtype=np.float32) / half)).astype(np.float32)
    pos = np.arange(S, dtype=np.float32)
    ang = (pos[None, :] * inv_freq[:, None]).astype(np.float32)
    cos = np.cos(ang).astype(np.float32)
    sin = np.sin(ang).astype(np.float32)
    c["ropec"] = np.tile(np.concatenate([cos, cos], 0), (2, 1))
    c["ropes"] = np.tile(np.concatenate([sin, -sin], 0), (2, 1))
    cend = (np.arange(255) * 16 + 31).astype(np.float32)
    angc = (cend[None, :] * inv_freq[:, None]).astype(np.float32)
    cc = np.zeros((64, 256), np.float32)
    ss = np.zeros((64, 256), np.float32)
    cc[:, :255] = np.concatenate([np.cos(angc)] * 2, 0)
    ss[:, :255] = np.concatenate([np.sin(angc), -np.sin(angc)], 0)
    c["cmpc"] = cc
    c["cmps"] = ss
    kk = np.arange(128)[:, None]
    tq = np.arange(512)[None, :]
    wm = np.zeros((128, 8, 512), np.float32)
    for rel in range(-4, 4):
        k = kk + 128 * rel
        wm[:, rel + 4, :] = ((k <= tq) & (k > tq - 512)).astype(np.float32)
    c["wm"] = wm.reshape(128, 8 * 512)
    n = (np.arange(2)[None, :, None] * 128 + np.arange(128)[:, None, None])
    t = np.arange(S)[None, None, :]
    c["cmaskT"] = ((16 * n + 31 <= t) & (n < 255)).astype(np.float32).reshape(128, 2 * S)
    c["eall"] = (np.arange(S)[None, :] // 64 == np.arange(64)[:, None]).astype(np.float32)
    ov = np.zeros((128, 2, 65), np.float32)
    for nt in range(2):
        for p in range(128):
            nn = nt * 128 + p
            if nn >= 255:
                continue
            for j in range(64):
                if 16 * nn < 64 * j + 64 and 16 * nn + 32 > 64 * j:
                    ov[p, nt, j] = 1.0
            ov[p, nt, 64] = 1.0
    c["ovaug"] = ov.reshape(128, 130)
    keep = np.zeros((128, 32, 64), np.float32)
    add = np.zeros((128, 32, 64), np.float32)
    for sub in range(32):
        for p in range(128):
            tt_ = sub * 128 + p
            cur = tt_ // 64
            for j in range(64):
                if j == cur:
                    add[p, sub, j] = 3e4
                elif j == cur - 1:
                    add[p, sub, j] = 2e4
                elif j == 0:
                    add[p, sub, j] = 1e4
                elif 64 * j > tt_:
                    add[p, sub, j] = -1.0 - j / 64.0
                else:
                    keep[p, sub, j] = 1.0
    c["keep"] = keep.reshape(128, 32 * 64)
    c["addm"] = add.reshape(128, 32 * 64)
    l_ = np.arange(128)[:, None, None] + 128 * np.arange(2)[None, :, None]
    c["tri"] = (l_ <= np.arange(256)[None, None, :]).astype(np.float32).reshape(128, 512)
    c["ident"] = np.eye(128, dtype=np.float32)
    return c


CONST_SHAPES = {"ropec": [128, S], "ropes": [128, S], "cmpc": [64, 256], "cmps": [64, 256], "wm": [128, 4096],
                "cmaskT": [128, 2 * S], "eall": [64, S], "ovaug": [128, 130], "keep": [128, 2048], "addm": [128, 2048],
                "tri": [128, 512], "ident": [128, 128]}

A_IN = 2048
B_IN = 2 * 1024 + 2 * 4 * 128 + 16
C_IN = 1024 + 6 * 256 + 48
OFF_B = A_IN
OFF_Z = OFF_B
OFF_XBC = OFF_B + 1024
OFF_DT = OFF_XBC + 2048
OFF_C = A_IN + B_IN
OFF_Q = OFF_C
OFF_KC = OFF_Q + 1024
OFF_VC = OFF_KC + 256
OFF_KS = OFF_VC + 256
OFF_VS = OFF_KS + 256
OFF_KW = OFF_VS + 256
OFF_VW = OFF_KW + 256
OFF_GL = OFF_VW + 256
OFF_G = A_IN + B_IN + C_IN


def prep_mixer_inputs(inputs, l, m):
    f = np.float32
    w = inputs["w_in"][l]

    def cw(cols):
        return chunkW(np.ascontiguousarray(w[:, cols]))
    a_val = cw(slice(0, 1024))
    a_gate = cw(slice(1024, 2048))
    m["winA%d" % l] = np.concatenate([a_val, a_gate], axis=2)
    singles = [cw(slice(OFF_Z, OFF_Z + 1024)), cw(slice(OFF_XBC, OFF_XBC + 2048)), cw(slice(OFF_Q, OFF_Q + 1024)),
               cw(slice(OFF_KC, OFF_KC + 256)), cw(slice(OFF_VC, OFF_VC + 256)), cw(slice(OFF_KS, OFF_KS + 256)),
               cw(slice(OFF_KW, OFF_KW + 256))]
    glw = np.zeros((1024, 128), f)
    glw[:, :48] = w[:, OFF_GL:OFF_GL + 48]
    singles.append(chunkW(glw))
    m["winS%d" % l] = np.concatenate(singles, axis=0)
    wv = np.concatenate([w[:, OFF_VS:OFF_VS + 256], w[:, OFF_VW:OFF_VW + 256]], 1)
    m["wvT%d" % l] = np.ascontiguousarray(wv.reshape(8, 128, 512).transpose(1, 0, 2)).reshape(128, 4096)
    wdt = w[:, OFF_DT:OFF_DT + 16]
    m["wdt%d" % l] = np.ascontiguousarray(wdt.reshape(8, 128, 16).transpose(1, 0, 2)).reshape(128, 128)
    m["wgate%d" % l] = chunkW(np.ascontiguousarray(w[:, OFF_G:OFF_G + 3072]))
    m["waout%d" % l] = chunkW(inputs["w_a_out"][l])
    m["wbout%d" % l] = chunkW(inputs["w_b_out"][l])
    m["wcout%d" % l] = chunkW(inputs["w_c_out"][l])
    m["wo%d" % l] = chunkW(inputs["w_o"][l])
    m["convaw%d" % l] = np.ascontiguousarray(inputs["conv_a_w"][l].T.reshape(8, 128, 31).transpose(1, 0, 2)).reshape(128, 248)
    m["convab%d" % l] = colvec(inputs["conv_a_b"][l])
    m["normag%d" % l] = colvec(inputs["norm_a_g"][l])
    m["normab%d" % l] = colvec(inputs["norm_a_b"][l])
    m["ssmcw%d" % l] = np.ascontiguousarray(inputs["ssm_conv_w"][l].T.reshape(16, 128, 4).transpose(1, 0, 2)).reshape(128, 64)
    m["ssmcb%d" % l] = colvec(inputs["ssm_conv_b"][l])
    m["dtb%d" % l] = np.tile(inputs["ssm_dt_bias"][l][None, :], (128, 1))
    m["alog%d" % l] = np.tile(inputs["ssm_a_log"][l][None, :], (128, 1))
    m["ssmd%d" % l] = colvec(np.repeat(inputs["ssm_d"][l], 64))
    m["ssmnw%d" % l] = colvec(inputs["ssm_norm_w"][l])
    for nm, key in (("k", "cmp_pe_k"), ("v", "cmp_pe_v")):
        m["pe%s%d" % (nm, l)] = np.ascontiguousarray(inputs[key][l].T)
    for nm, key in (("k", "cmp_k_w1"), ("v", "cmp_v_w1")):
        m["w1%s%d" % (nm, l)] = np.ascontiguousarray(inputs[key][l].reshape(32, 64, 128).transpose(1, 0, 2)).reshape(64, 4096)
    m["w2k%d" % l] = inputs["cmp_k_w2"][l]
    m["w2v%d" % l] = inputs["cmp_v_w2"][l]


MIX_SHAPES = {"winA": [8, 128, 2048], "winS": [41, 128, 1024], "wvT": [128, 4096], "wdt": [128, 128], "wgate": [24, 128, 1024],
              "waout": [8, 128, 1024], "wbout": [8, 128, 1024], "wcout": [8, 128, 1024], "wo": [8, 128, 1024],
              "convaw": [128, 248], "convab": [128, 8], "normag": [128, 8], "normab": [128, 8], "ssmcw": [128, 64],
              "ssmcb": [128, 16], "dtb": [128, 16], "alog": [128, 16], "ssmd": [128, 8], "ssmnw": [128, 8],
              "pek": [64, 32], "pev": [64, 32], "w1k": [64, 4096], "w1v": [64, 4096], "w2k": [128, 64], "w2v": [128, 64]}


def phase_m1(E, l):
    kb, A, P = E["kb"], E["kb"].A, E["kb"].P
    d = E["lay"][l]
    sc = E["scr"]
    C = E["consts"]
    mo, mob = E["mods"][l]
    m1, m1b = E["mod1"][l]
    hT = E["hT"]
    mk = A.mark()
    kb.nrot = 8
    u, ub = kb.sb([128, 8, S], BF16, "u")
    def ld(name, shape, src):
        t, b = kb.sb(shape, F32, name, dma=True)
        kb.dma(t[:], src.ap(), b, (), [b])
        return t, b
    caw, cawb = ld("caw", [128, 248], d["convaw"])
    cab, cabb = ld("cab", [128, 8], d["convab"])
    scw, scwb = ld("scw", [128, 64], d["ssmcw"])
    scb_, scbb = ld("scb", [128, 16], d["ssmcb"])
    dtb, dtbb = ld("dtb", [128, 16], d["dtb"])
    dt_sb, dt_b = E["dt"]
    mk2 = A.mark()
    hst = [kb.sb([128, 1024], F32, "hst", dma=True) for _ in range(2)]
    i = 0
    for k in range(8):
        for hh in range(4):
            st, stb = hst[i % 2]
            i += 1
            kb.dma(st[:], hT.ap()[k * 128:(k + 1) * 128, hh * 1024:(hh + 1) * 1024], stb, (), [stb])
            kb.act(u[:, k, hh * 1024:(hh + 1) * 1024], st[:], AF.Identity, [stb, mob, m1b], [ub],
                   bias=mo[:, 24 + k:24 + k + 1], scale=m1[:, 32 + k:32 + k + 1])
    wv32, wv32b = kb.sb([128, 4096], F32, "wv32", dma=True)
    kb.dma(wv32[:], d["wvT"].ap(), wv32b, (), [wv32b])
    wvb, wvbb = kb.sb([128, 4096], BF16, "wvb")
    kb.cp("pool", wvb[:], wv32[:], [wv32b], [wvbb])
    wd32, wd32b = kb.sb([128, 128], F32, "wd32", dma=True)
    kb.dma(wd32[:], d["wdt"].ap(), wd32b, (), [wd32b])
    wdb, wdbb = kb.sb([128, 128], BF16, "wdb")
    kb.cp("pool", wdb[:], wd32[:], [wd32b], [wdbb])
    vo = [kb.sb([128, 512], BF16, "vo", dma=True) for _ in range(2)]
    dtt, dttb = kb.sb([128, 16], F32, "dtt")
    for tcn in range(32):
        tsl = slice(tcn * 128, (tcn + 1) * 128)
        pv, pvb = kb.psum()
        for k in range(8):
            kb.mm(pv[:], u[:, k, tsl], wvb[:, k * 512:(k + 1) * 512], k == 0, k == 7, [ub, wvbb], [pvb])
        o, ob = vo[tcn % 2]
        kb.cp("act", o[:], pv[:], [pvb], [ob])
        kb.dma(sc["vsw"].ap()[tsl, :], o[:], ob, [ob], ())
        pd, pdb = kb.psum()
        for k in range(8):
            kb.mm(pd[:, 0:16], u[:, k, tsl], wdb[:, k * 16:(k + 1) * 16], k == 0, k == 7, [ub, wdbb], [pdb])
        kb.tt("dve", dtt[:], pd[:, 0:16], dtb[:], ALU.add, [pdb, dtbb], [dttb])
        kb.act(dtt[:], dtt[:], AF.Exp, [dttb], [dttb])
        kb.act(dt_sb[:, tcn, :], dtt[:], AF.Ln, [dttb], [dt_b], bias=1.0)
    P.barrier()
    A.release(mk2)
    ropec, ropecb = ld("ropec", [128, S], C["ropec"])
    ropes, ropesb = ld("ropes", [128, S], C["ropes"])
    ws = WStream(kb, 2048, 2, "win")
    ga = [kb.sb([128, 30 + S], F32, "ga") for _ in range(2)]
    acc = [kb.sb([128, S], F32, "acc", dma=True) for _ in range(1)]
    accz = [kb.sb([128, S], F32, "accz", dma=True) for _ in range(1)]
    rowb = [kb.sb([128, S], BF16, "rowb", dma=True) for _ in range(1)]
    tmp = [kb.sb([128, 512], F32, "tmp") for _ in range(2)]
    tmp2 = [kb.sb([128, 512], F32, "tmp2") for _ in range(2)]
    for g_, gb_ in ga:
        kb.memset("pool", g_[:, 0:30], 0.0, [gb_])
    others = [("z", c) for c in range(8)] + [("xbc", c) for c in range(16)] + \
             [("q", c) for c in range(8)] + [("kc", c) for c in range(2)] + [("vc", c) for c in range(2)] + \
             [("ks", c) for c in range(2)] + [("kw", c) for c in range(2)] + [("gl", 0)]
    items = []
    oi = 0
    for c in range(8):
        items.append(("A", c))
        for _ in range(5):
            if oi < len(others):
                items.append(others[oi])
                oi += 1
    items += others[oi:]

    def src_of(it):
        kind, c = it
        if kind == "A":
            return d["winA"].ap()[c], 2048
        base = {"z": 0, "xbc": 8, "q": 24, "kc": 32, "vc": 34, "ks": 36, "kw": 38, "gl": 40}[kind]
        return d["winS"].ap()[base + c], 1024
    s0, n0 = src_of(items[0])
    ws.prefetch(s0, n0)
    cnt = {"ga": 0, "acc": 0, "rowb": 0, "tmp": 0, "tmp2": 0}

    def nxt(lst, key):
        r = lst[cnt[key] % len(lst)]
        cnt[key] += 1
        return r

    def rope_tile(ps, psb, out_ap, outb, t0):
        x, xb = nxt(tmp, "tmp")
        kb.cp("act", x[:], ps[:], [psb], [xb])
        t1, t1b = nxt(tmp2, "tmp2")
        kb.tt("dve", t1[:], x[:], ropec[:, t0:t0 + 512], ALU.mult, [xb, ropecb], [t1b])
        t2, t2b = nxt(tmp, "tmp")
        for q4 in range(4):
            src = (q4 ^ 1) * 32
            kb.tt("pool", t2[q4 * 32:(q4 + 1) * 32, :], x[src:src + 32, :], ropes[src:src + 32, t0:t0 + 512],
                  ALU.mult, [xb, ropesb], [t2b])
        kb.tt("dve", out_ap, t1[:], t2[:], ALU.add, [t1b, t2b], [outb])

    for ii, it in enumerate(items):
        kind, c = it
        if ii + 1 < len(items):
            s1, n1 = src_of(items[ii + 1])
            ws.prefetch(s1, n1)
        w, wb = ws.get("pool")
        if kind == "A":
            g_, gb_ = nxt(ga, "ga")
            for t in range(8):
                t0 = t * 512
                pv, pvb = kb.psum()
                pg, pgb = kb.psum()
                for k in range(8):
                    kb.mm(pv[:], w[:, k * 128:(k + 1) * 128], u[:, k, t0:t0 + 512], k == 0, k == 7, [wb, ub], [pvb])
                for k in range(8):
                    kb.mm(pg[:], w[:, 1024 + k * 128:1024 + (k + 1) * 128], u[:, k, t0:t0 + 512], k == 0, k == 7, [wb, ub], [pgb])
                x, xb = nxt(tmp, "tmp")
                kb.act(x[:], pg[:], AF.Sigmoid, [pgb], [xb])
                kb.tt("dve", g_[:, 30 + t0:30 + t0 + 512], x[:], pv[:], ALU.mult, [xb, pvb], [gb_])
            a_, ab_ = nxt(acc, "acc")
            kb.ts("dve", a_[:], g_[:, 0:S], caw[:, c * 31:c * 31 + 1], cab[:, c:c + 1], ALU.mult, ALU.add, [gb_, cawb, cabb], [ab_])
            for k in range(1, 31):
                kb.stt(a_[:], g_[:, k:k + S], caw[:, c * 31 + k:c * 31 + k + 1], a_[:], ALU.mult, ALU.add, [gb_, cawb, ab_], [ab_])
            kb.dma(sc["aconvT"].ap()[c * 128:(c + 1) * 128, :], a_[:], ab_, [ab_], ())
            continue
        if kind == "xbc":
            g_, gb_ = nxt(ga, "ga")
        elif kind in ("z", "gl"):
            a_, ab_ = accz[0]
        else:
            r_, rb_ = nxt(rowb, "rowb")
        for t in range(8):
            t0 = t * 512
            pv, pvb = kb.psum()
            for k in range(8):
                kb.mm(pv[:], w[:, k * 128:(k + 1) * 128], u[:, k, t0:t0 + 512], k == 0, k == 7, [wb, ub], [pvb])
            if kind == "z":
                kb.act(a_[:, t0:t0 + 512], pv[:], AF.Silu, [pvb], [ab_])
            elif kind == "gl":
                kb.act(a_[:, t0:t0 + 512], pv[:], AF.Sigmoid, [pvb], [ab_])
            elif kind == "xbc":
                kb.cp("act", g_[:, 30 + t0:30 + t0 + 512], pv[:], [pvb], [gb_])
            elif kind in ("kc", "vc"):
                kb.cp("act", r_[:, t0:t0 + 512], pv[:], [pvb], [rb_])
            else:
                rope_tile(pv, pvb, r_[:, t0:t0 + 512], rb_, t0)
        if kind == "z":
            kb.dma(sc["zsT"].ap()[c * 128:(c + 1) * 128, :], a_[:], ab_, [ab_], ())
        elif kind == "gl":
            kb.dma(sc["glT"].ap(), a_[:], ab_, [ab_], ())
        elif kind == "xbc":
            a_, ab_ = nxt(acc, "acc")
            kb.ts("dve", a_[:], g_[:, 27:27 + S], scw[:, c * 4:c * 4 + 1], scb_[:, c:c + 1], ALU.mult, ALU.add, [gb_, scwb, scbb], [ab_])
            for k in range(1, 4):
                kb.stt(a_[:], g_[:, 27 + k:27 + k + S], scw[:, c * 4 + k:c * 4 + k + 1], a_[:], ALU.mult, ALU.add, [gb_, scwb, ab_], [ab_])
            kb.act(a_[:], a_[:], AF.Silu, [ab_], [ab_])
            dst = sc["xsT"].ap()[c * 128:(c + 1) * 128, :] if c < 8 else (
                sc["BT"].ap()[(c - 8) * 128:(c - 7) * 128, :] if c < 12 else sc["CT"].ap()[(c - 12) * 128:(c - 11) * 128, :])
            kb.dma(dst, a_[:], ab_, [ab_], ())
        else:
            dst = {"q": sc["qT"], "kc": sc["kcT"], "vc": sc["vcT"], "ks": sc["ksT"], "kw": sc["kwT"]}[kind]
            kb.dma(dst.ap()[c * 128:(c + 1) * 128, :], r_[:], rb_, [rb_], ())
    P.barrier()
    A.release(mk)


def phase_ssd(E, l):
    kb, A, P = E["kb"], E["kb"].A, E["kb"].P
    d = E["lay"][l]
    sc = E["scr"]
    C = E["consts"]
    mk = A.mark()
    kb.nrot = 8
    dt_sb, dt_b = E["dt"]

    def ld(name, shape, src):
        t, b = kb.sb(shape, F32, name, dma=True)
        o = t[:]
        if len(shape) == 3:
            o = o.rearrange("p a b -> p (a b)")
        kb.dma(o, src.ap(), b, (), [b])
        return t, b
    tri, trib = ld("tri", [128, 2, 256], C["tri"])
    ident, identb = ld("ident", [128, 128], C["ident"])
    alog, alogb = ld("alog", [128, 16], d["alog"])
    Dc, Dcb = ld("Dc", [128, 8], d["ssmd"])
    negA, negAb = kb.sb([128, 16], F32, "negA")
    kb.act(negA[:], alog[:], AF.Exp, [alogb], [negAb])
    kb.ts("dve", negA[:], negA[:], -1.0, None, ALU.mult, None, [negAb], [negAb])
    H, Hb_ = kb.sb([128, 1024], F32, "H")
    Hbf, Hbfb = kb.sb([128, 1024], BF16, "Hbf")
    kb.memset("dve", H[:], 0.0, [Hb_])
    kb.memset("dve", Hbf[:], 0.0, [Hbfb])

    def two(shape, dt_, name, dma=False):
        return [kb.sb(shape, dt_, name, dma=dma) for _ in range(2)]
    xs_s = two([128, 8, 256], F32, "xs", True)
    zs1 = kb.sb([128, 8, 256], F32, "zs", dma=True)
    Bt_s = two([128, 4, 256], F32, "Bt", True)
    Ct_s = two([128, 4, 256], F32, "Ct", True)
    Btb_s = two([128, 4, 256], BF16, "Btb")
    Ctb_s = two([128, 4, 256], BF16, "Ctb")
    a_t_s = two([128, 2, 16], F32, "a_t")
    acsT_s = two([128, 2, 16], F32, "acsT")
    abc, abcb = kb.sb([128, 2, 16, 128], F32, "abc")
    acsR_s = two([128, 16, 256], F32, "acsR")
    eR_s = two([128, 16, 256], F32, "eR")
    di_s = two([128, 2, 16], F32, "di")
    dtd_s = two([128, 2, 16], F32, "dtd")
    cbm_s = two([128, 4, 2, 256], F32, "cbm")
    xp_s = two([128, 2, 1024], BF16, "xp")
    xpd_s = two([128, 2, 1024], BF16, "xpd")
    Bn_s = two([128, 2, 512], BF16, "Bn")
    NSL = 3
    dif = [kb.sb([128, 256], F32, "dif") for _ in range(2 * NSL)]
    Mt = [kb.sb([128, 2, 256], BF16, "Mt") for _ in range(NSL)]
    Cs = [kb.sb([128, 256], BF16, "Cs") for _ in range(NSL)]
    yo = two([128, 8, 256], F32, "yo", True)
    for (m_, mb_) in Mt:
        kb.memset("pool", m_[:], 0.0, [mb_])

    def load(c):
        s = c % 2
        csl = slice(c * 256, (c + 1) * 256)
        for k in range(8):
            kb.dma(xs_s[s][0][:, k, :], sc["xsT"].ap()[k * 128:(k + 1) * 128, csl], xs_s[s][1], (), [xs_s[s][1]])
        for g in range(4):
            kb.dma(Bt_s[s][0][:, g, :], sc["BT"].ap()[g * 128:(g + 1) * 128, csl], Bt_s[s][1], (), [Bt_s[s][1]])
            kb.dma(Ct_s[s][0][:, g, :], sc["CT"].ap()[g * 128:(g + 1) * 128, csl], Ct_s[s][1], (), [Ct_s[s][1]])

    def preamble(c):
        s = c % 2
        xs, xsb = xs_s[s]
        Bt, Btb_ = Bt_s[s]
        Ct, Ctb_ = Ct_s[s]
        Btb, Btbb = Btb_s[s]
        Ctb, Ctbb = Ctb_s[s]
        a_t, a_tb = a_t_s[s]
        acsT, acsTb = acsT_s[s]
        acsR, acsRb = acsR_s[s]
        eR, eRb = eR_s[s]
        di, dib = di_s[s]
        dtd, dtdb = dtd_s[s]
        cbm, cbmb = cbm_s[s]
        xp, xpb = xp_s[s]
        xpd, xpdb = xpd_s[s]
        Bn, Bnb = Bn_s[s]
        for lt in range(2):
            kb.tt("dve", a_t[:, lt, :], dt_sb[:, 2 * c + lt, :], negA[:], ALU.mult, [dt_b, negAb], [a_tb])
        for lo in range(2):
            ps, psb = kb.psum()
            for lt in range(lo + 1):
                kb.mm(ps[:, 0:16], tri[:, lt, lo * 128:(lo + 1) * 128], a_t[:, lt, :], lt == 0, lt == lo, [trib, a_tb], [psb])
            kb.cp("act", acsT[:, lo, :], ps[:, 0:16], [psb], [acsTb])
        for lt in range(2):
            kb.cp("pool", abc[:, lt, :, :], a_t[:, lt, :].unsqueeze(2).to_broadcast([128, 16, 128]), [a_tb], [abcb])
        yield
        for hp in range(8):
            ps, psb = kb.psum()
            for hh in range(2):
                h = 2 * hp + hh
                for lt in range(2):
                    kb.mm(ps[:, hh * 256:(hh + 1) * 256], abc[:, lt, h, :], tri[:, lt, :], lt == 0, lt == 1, [abcb, trib], [psb])
            kb.cp("act", acsR[:, 2 * hp:2 * hp + 2, :], ps[:].rearrange("p (a b) -> p a b", a=2), [psb], [acsRb])
            yield
        kb.act(eR[:], acsR[:], AF.Exp, [acsRb], [eRb])
        for lt in range(2):
            kb.tt("dve", di[:, lt, :], acsR[:, :, 255], acsT[:, lt, :], ALU.subtract, [acsRb, acsTb], [dib])
        kb.act(di[:], di[:], AF.Exp, [dib], [dib])
        for lt in range(2):
            kb.tt("dve", dtd[:, lt, :], di[:, lt, :], dt_sb[:, 2 * c + lt, :], ALU.mult, [dib, dt_b], [dtdb])
        yield
        kb.cp("pool", Btb[:], Bt[:], [Btb_], [Btbb])
        kb.cp("pool", Ctb[:], Ct[:], [Ctb_], [Ctbb])
        for g in range(4):
            for st in range(2):
                ps, psb = kb.psum()
                kb.mm(ps[:, 0:256], Btb[:, g, st * 128:(st + 1) * 128], Ctb[:, g, :], True, True, [Btbb, Ctbb], [psb])
                kb.tt("dve", cbm[:, g, st, :], ps[:, 0:256], tri[:, st, :], ALU.mult, [psb, trib], [cbmb])
        yield
        for k in range(8):
            for lt in range(2):
                ps, psb = kb.psum()
                kb.tr(ps[:, 0:128], xs[:, k, lt * 128:(lt + 1) * 128], ident[:], [xsb, identb], [psb])
                for hh in range(2):
                    h = 2 * k + hh
                    kb.act(xp[:, lt, h * 64:(h + 1) * 64], ps[:, hh * 64:(hh + 1) * 64], AF.Identity, [psb, dt_b], [xpb],
                           scale=dt_sb[:, 2 * c + lt, h:h + 1])
                    kb.ts("dve", xpd[:, lt, h * 64:(h + 1) * 64], ps[:, hh * 64:(hh + 1) * 64], dtd[:, lt, h:h + 1], None,
                          ALU.mult, None, [psb, dtdb], [xpdb])
            if k % 2 == 1:
                yield
        for g in range(4):
            for lt in range(2):
                ps, psb = kb.psum()
                kb.tr(ps[:, 0:128], Bt[:, g, lt * 128:(lt + 1) * 128], ident[:], [Btb_, identb], [psb])
                kb.cp("act", Bn[:, lt, g * 128:(g + 1) * 128], ps[:, 0:128], [psb], [Bnb])
        yield

    def run_all(gen):
        for _ in gen:
            pass

    load(0)
    run_all(preamble(0))
    cnt = [0]
    for c in range(16):
        s = c % 2
        gen = None
        if c + 1 < 16:
            load(c + 1)
            gen = preamble(c + 1)
            if SSDVAR == 1:
                run_all(gen)
                gen = None
        zs, zsb = zs1
        csl = slice(c * 256, (c + 1) * 256)
        for k in range(8):
            kb.dma(zs[:, k, :], sc["zsT"].ap()[k * 128:(k + 1) * 128, csl], zsb, (), [zsb])
        xs, xsb = xs_s[s]
        Ct, Ctb_ = Ct_s[s]
        acsT, acsTb = acsT_s[s]
        acsR, acsRb = acsR_s[s]
        eR, eRb = eR_s[s]
        cbm, cbmb = cbm_s[s]
        xp, xpb = xp_s[s]
        xpd, xpdb = xpd_s[s]
        Bn, Bnb = Bn_s[s]
        y_, yb_ = yo[s]
        for h in range(16):
            g = h // 4
            i = cnt[0]
            cnt[0] += 1
            m_, mb_ = Mt[i % NSL]
            for st in range(2):
                lo = st * 128
                df, dfb = dif[(i % NSL) * 2 + st]
                kb.ts("dve", df[:, lo:256], acsR[:, h, lo:256], acsT[:, st, h:h + 1], 0.0, ALU.subtract, ALU.min,
                      [acsRb, acsTb], [dfb])
                kb.act(df[:, lo:256], df[:, lo:256], AF.Exp, [dfb], [dfb])
                kb.tt("pool", m_[:, st, lo:256], df[:, lo:256], cbm[:, g, st, lo:256], ALU.mult, [dfb, cbmb], [mb_])
            cs, csb = Cs[i % NSL]
            kb.tt("pool", cs[:], Ct[:, g, :], eR[:, h, :], ALU.mult, [Ctb_, eRb], [csb])
            ps, psb = kb.psum()
            kb.mm(ps[0:64, 0:256], xp[:, 0, h * 64:(h + 1) * 64], m_[:, 0, :], True, False, [xpb, mb_], [psb])
            kb.mm(ps[0:64, 0:256], xp[:, 1, h * 64:(h + 1) * 64], m_[:, 1, :], False, False, [xpb, mb_], [psb])
            kb.mm(ps[0:64, 0:256], Hbf[:, h * 64:(h + 1) * 64], cs[:], False, True, [Hbfb, csb], [psb])
            hh = h % 2
            kb.cp("act", y_[hh * 64:(hh + 1) * 64, h // 2, :], ps[0:64, 0:256], [psb], [yb_])
            if gen is not None and SSDVAR == 0:
                next(gen, None)
        if gen is not None:
            run_all(gen)
        for k in range(8):
            kb.stt(y_[:, k, :], xs[:, k, :], Dc[:, k:k + 1], y_[:, k, :], ALU.mult, ALU.add, [xsb, Dcb, yb_], [yb_])
            kb.tt("pool", y_[:, k, :], y_[:, k, :], zs[:, k, :], ALU.mult, [yb_, zsb], [yb_])
            kb.dma(sc["ybT"].ap()[k * 128:(k + 1) * 128, csl], y_[:, k, :], yb_, [yb_], ())
        for g in range(4):
            ps, psb = kb.psum()
            for lt in range(2):
                kb.mm(ps[:, 0:256], Bn[:, lt, g * 128:(g + 1) * 128], xpd[:, lt, g * 256:(g + 1) * 256], lt == 0, lt == 1,
                      [Bnb, xpdb], [psb])
            for r in range(4):
                h = 4 * g + r
                kb.stt(H[:, h * 64:(h + 1) * 64], H[:, h * 64:(h + 1) * 64], eR[:, h, 255:256], ps[:, r * 64:(r + 1) * 64],
                       ALU.mult, ALU.add, [Hb_, eRb, psb], [Hb_])
        kb.cp("pool", Hbf[:], H[:], [Hb_], [Hbfb])
    P.barrier()
    A.release(mk)


GELU_C = math.sqrt(2.0 / math.pi)


def phase_nsa(E, l):
    kb, A, P = E["kb"], E["kb"].A, E["kb"].P
    d = E["lay"][l]
    sc = E["scr"]
    C = E["consts"]
    ones, onesb = E["ones"]
    mk = A.mark()
    kb.nrot = 5
    PO = [(kb.ps[5], kb.psb[5]), (kb.ps[6], kb.psb[6]), (kb.ps[7], kb.psb[7])]
    poi = [0]

    def next_po():
        r = PO[poi[0] % 3]
        poi[0] += 1
        return r

    def ld(name, shape, src, dtype=F32):
        t, b = kb.sb(shape, dtype, name, dma=True)
        o = t[:]
        if len(shape) == 3:
            o = o.rearrange("p a b -> p (a b)")
        kb.dma(o, src if not hasattr(src, "ap") else src.ap(), b, (), [b])
        return t, b

    def ld_bf(name, shape, src, stg, dest):
        t, b = dest
        o = t[:]
        if len(shape) == 3:
            o = o.rearrange("p a b -> p (a b)")
        npart = shape[0]
        n = int(np.prod(shape[1:]))
        st, stb = stg
        for c0 in range(0, n, 4096):
            c1 = min(n, c0 + 4096)
            kb.dma(st[0:npart, 0:c1 - c0], src.ap()[:, c0:c1], stb, (), [stb])
            kb.cp("pool", o[:, c0:c1], st[0:npart, 0:c1 - c0], [stb], [b])
        return t, b

    ident, identb = ld("ident", [128, 128], C["ident"])
    wm_d = kb.sb([128, 8, 512], BF16, "wm")
    cmT_d = kb.sb([128, 2, S], BF16, "cmT")
    eall_d = kb.sb([64, S], BF16, "eall")
    ov_d = kb.sb([128, 2, 65], BF16, "ov")
    w1_d = {"k": kb.sb([64, 32, 128], BF16, "w1k"), "v": kb.sb([64, 32, 128], BF16, "w1v")}
    mk_stg = A.mark()
    stg = kb.sb([128, 4096], F32, "stg", dma=True)
    wm, wmb = ld_bf("wm", [128, 8, 512], C["wm"], stg, wm_d)
    cmT, cmTb = ld_bf("cmT", [128, 2, S], C["cmaskT"], stg, cmT_d)
    eall, eallb = ld_bf("eall", [64, S], C["eall"], stg, eall_d)
    ovb, ovbb = ld_bf("ov", [128, 2, 65], C["ovaug"], stg, ov_d)
    w1 = {}
    for nm in ("k", "v"):
        w1[nm] = ld_bf("w1" + nm, [64, 32, 128], d["w1" + nm], stg, w1_d[nm])
    P.barrier()
    A.release(mk_stg)
    cmpc, cmpcb = ld("cmpc", [64, 256], C["cmpc"])
    cmps, cmpsb = ld("cmps", [64, 256], C["cmps"])
    gl_s = [kb.sb([48, 512], F32, "gl", dma=True) for _ in range(2)]
    selg, selgb = kb.sb([48, 48, 64], BF16, "selg")
    kb.cp("dve", selg[:], ident[0:48, 0:48].unsqueeze(2).to_broadcast([48, 48, 64]), [identb], [selgb])
    w2 = {}
    pe = {}
    for nm in ("k", "v"):
        t32, t32b = ld("w2" + nm, [128, 64], d["w2" + nm])
        tb, tbb = kb.sb([128, 64], BF16, "w2b")
        kb.cp("pool", tb[:], t32[:], [t32b], [tbb])
        w2[nm] = (tb, tbb)
        p32, p32b = ld("pe" + nm, [64, 32], d["pe" + nm])
        pb, pbb = kb.sb([64, 32], BF16, "peb")
        kb.cp("pool", pb[:], p32[:], [p32b], [pbb])
        pe[nm] = (pb, pbb)
    ks, ksb = kb.sb([128, S], BF16, "ks", dma=True)
    kw, kwb = kb.sb([128, S], BF16, "kw", dma=True)
    vs, vsb = kb.sb([128, 32, 128], BF16, "vs", dma=True)
    vw, vwb = kb.sb([128, 32, 128], BF16, "vw", dma=True)
    kc, kcb = kb.sb([64, S], BF16, "kc", dma=True)
    vc, vcb = kb.sb([64, S], BF16, "vc", dma=True)
    kcmp, kcmpb = kb.sb([128, 256], BF16, "kcmp")
    vcmp, vcmpb = kb.sb([128, 2, 128], BF16, "vcmp")
    h1 = [kb.sb([128, 256], F32, "h1") for _ in range(2)]
    h1b_ = kb.sb([128, 256], BF16, "h1b")
    gtmp = [kb.sb([128, 256], F32, "gt") for _ in range(2)]
    biasc, biascb = kb.sb([128, 1], F32, "biasc")
    qt_s = [kb.sb([128, 2, 512], BF16, "qt", dma=True) for _ in range(2)]
    Es = [kb.sb([128, 512], BF16, "E") for _ in range(14)]
    ei = [0]
    selm, selmb = kb.sb([128, 32, 512], BF16, "selm")
    selT, selTb = kb.sb([64, 512], BF16, "selT")
    impa, impab = kb.sb([128, 4, 64], F32, "impa")
    imps, impsb = kb.sb([128, 64], F32, "imps")
    imp2, imp2b = kb.sb([128, 64], F32, "imp2")
    sel4, sel4b = kb.sb([128, 4, 64], F32, "sel4")
    m8, m8b = kb.sb([128, 8], F32, "m8")
    thr, thrb = kb.sb([128, 1], F32, "thr")
    rden, rdenb = kb.sb([128, 1], F32, "rden")
    glh_s = [kb.sb([48, 512], BF16, "glh") for _ in range(2)]
    gll_s = [kb.sb([48, 512], BF16, "gll") for _ in range(2)]
    gld, gldb = kb.sb([48, 512], F32, "gld")
    wgt, wgtb = kb.sb([64, 512], F32, "wgt")
    tiny, tinyb = kb.sb([128, 1], F32, "tiny")
    kb.memset("dve", tiny[:], 1e-30, [tinyb])
    otmp, otmpb = kb.sb([64, 512], F32, "otmp")
    oaccs = [kb.sb([64, 512], F32, "oacc") for _ in range(4)]
    keep_s = [kb.sb([128, 4, 64], F32, "keep", dma=True) for _ in range(2)]
    addm_s = [kb.sb([128, 4, 64], F32, "addm", dma=True) for _ in range(2)]
    obuf = [kb.sb([128, 512], BF16, "obuf", dma=True) for _ in range(2)]
    kb.memset("dve", vs[:, :, 64:128], 1.0, [vsb])
    kb.memset("dve", vw[:, :, 64:128], 1.0, [vwb])
    kb.memset("dve", vcmp[:], 0.0, [vcmpb])
    kb.memset("dve", vcmp[:, 0, 64:128], 1.0, [vcmpb])
    kb.memset("dve", vcmp[0:127, 1, 64:128], 1.0, [vcmpb])
    kb.memset("dve", kcmp[:], 0.0, [kcmpb])

    def gelu_tanh(x, xb, out, outb):
        t, tb = gtmp[0]
        t2, t2b = gtmp[1]
        kb.tt("dve", t[:], x[:], x[:], ALU.mult, [xb], [tb])
        kb.ts("dve", t[:], t[:], 0.044715, 1.0, ALU.mult, ALU.add, [tb], [tb])
        kb.tt("dve", t[:], t[:], x[:], ALU.mult, [tb, xb], [tb])
        kb.act(t2[:], t[:], AF.Tanh, [tb], [t2b], scale=GELU_C)
        kb.ts("dve", t2[:], t2[:], 1.0, 0.5, ALU.add, ALU.mult, [t2b], [t2b])
        kb.tt("dve", out, t2[:], x[:], ALU.mult, [t2b, xb], [outb])

    LA = 10
    pipe = {"q": [], "step": 0}

    def p_add(delay, fn):
        pipe["q"].append((pipe["step"] + delay, fn))

    def p_tick():
        pipe["step"] += 1
        rest = []
        for due, fn in pipe["q"]:
            if due <= pipe["step"]:
                fn()
            else:
                rest.append((due, fn))
        pipe["q"] = rest

    def p_flush():
        while pipe["q"]:
            p_tick()

    def attend(kT, kTb, half, kcol, vaug_ap, vb, mask_ap, maskb, q_ap, qb, po, pob, first, last, mi):
        ps, psb = kb.psum()
        kb.mm(ps[:], kT[half * 64:(half + 1) * 64, kcol:kcol + 128], q_ap, True, True, [kTb, qb], [psb])
        e, eb = Es[ei[0] % len(Es)]
        ei[0] += 1
        kb.act(e[:], ps[:], AF.Exp, [psb], [eb], scale=0.125)
        kb.tt("pool" if mi % 3 == 2 else "dve", e[:], e[:], mask_ap, ALU.mult, [eb, maskb], [eb])
        p_add(LA, lambda: kb.mm(po[:, :], vaug_ap, e[:], first, last, [vb, eb], [pob]))
        p_tick()
        return e, eb

    def finish_branch(po, pob, grow, first, glt, oacc, oaccb):
        glh, glhb, gll, gllb = glt

        def rest():
            kb.act(wgt[:], po[64:128, :], AF.Ln, [pob, tinyb], [wgtb], bias=tiny[64:128, 0:1])
            kb.act(wgt[:], wgt[:], AF.Exp, [wgtb], [wgtb], scale=-1.0)
            pg, pgb = kb.psum()
            kb.mm(pg[0:64, :], selg[:, grow, :], glh[:, :], True, False, [selgb, glhb], [pgb])
            kb.mm(pg[0:64, :], selg[:, grow, :], gll[:, :], False, True, [selgb, gllb], [pgb])
            kb.tt("dve", wgt[:], wgt[:], pg[0:64, :], ALU.mult, [wgtb, pgb], [wgtb])
            if first:
                kb.tt("dve", oacc[:], po[0:64, :], wgt[:], ALU.mult, [pob, wgtb], [oaccb])
            else:
                kb.tt("dve", otmp[:], po[0:64, :], wgt[:], ALU.mult, [pob, wgtb], [otmpb])
                kb.tt("pool", oacc[:], oacc[:], otmp[:], ALU.add, [oaccb, otmpb], [oaccb])
        p_add(LA + 1, rest)
        p_tick()

    qi = 0
    for g in range(4):
        gr = slice(g * 64, (g + 1) * 64)
        for hf in range(2):
            kb.dma(ks[hf * 64:(hf + 1) * 64, :], sc["ksT"].ap()[gr, :], ksb, (), [ksb])
            kb.dma(kw[hf * 64:(hf + 1) * 64, :], sc["kwT"].ap()[gr, :], kwb, (), [kwb])
        kb.dma(kc[:], sc["kcT"].ap()[gr, :], kcb, (), [kcb])
        kb.dma(vc[:], sc["vcT"].ap()[gr, :], vcb, (), [vcb])
        for ch in range(0, 32, 8):
            kb.dma(vs[:, ch:ch + 8, 0:64], sc["vsw"].ap()[ch * 128:(ch + 8) * 128, g * 64:(g + 1) * 64].rearrange("(c p) d -> p c d", p=128),
                   vsb, (), [vsb])
            kb.dma(vw[:, ch:ch + 8, 0:64], sc["vsw"].ap()[ch * 128:(ch + 8) * 128, 256 + g * 64:256 + (g + 1) * 64].rearrange("(c p) d -> p c d", p=128),
                   vwb, (), [vwb])
        for nm, src, srcb in (("k", kc, kcb), ("v", vc, vcb)):
            w1t, w1b_ = w1[nm]
            pb, pbb = pe[nm]
            w2t, w2b_ = w2[nm]
            pbias, pbiasb = kb.psum()
            for j in range(32):
                kb.mm(pbias[:, 0:1], w1t[:, j, :], pb[:, j:j + 1], j == 0, j == 31, [w1b_, pbb], [pbiasb])
            kb.cp("act", biasc[:], pbias[:, 0:1], [pbiasb], [biascb])
            ph, phb = kb.psum()
            for j in range(32):
                kb.mm(ph[:, 0:255], w1t[:, j, :], src[:, j:j + 16 * 254 + 1:16], j == 0, j == 31, [w1b_, srcb], [phb])
            x, xb = h1[0]
            kb.act(x[:, 0:255], ph[:, 0:255], AF.Identity, [phb, biascb], [xb], bias=biasc[:, 0:1])
            hb, hbb = h1b_
            gelu_tanh_in = (x, xb)
            kb.memset("pool", hb[:, 255:256], 0.0, [hbb])
            t, tb = gtmp[0]
            t2, t2b = gtmp[1]
            kb.tt("dve", t[:, 0:255], x[:, 0:255], x[:, 0:255], ALU.mult, [xb], [tb])
            kb.ts("dve", t[:, 0:255], t[:, 0:255], 0.044715, 1.0, ALU.mult, ALU.add, [tb], [tb])
            kb.tt("dve", t[:, 0:255], t[:, 0:255], x[:, 0:255], ALU.mult, [tb, xb], [tb])
            kb.act(t2[:, 0:255], t[:, 0:255], AF.Tanh, [tb], [t2b], scale=GELU_C)
            kb.ts("dve", t2[:, 0:255], t2[:, 0:255], 1.0, 0.5, ALU.add, ALU.mult, [t2b], [t2b])
            kb.tt("dve", hb[:, 0:255], t2[:, 0:255], x[:, 0:255], ALU.mult, [t2b, xb], [hbb])
            if nm == "k":
                pk, pkb = kb.psum()
                kb.mm(pk[0:64, 0:255], w2t[:], hb[:, 0:255], True, True, [w2b_, hbb], [pkb])
                y, yb = h1[1]
                kb.cp("act", y[0:64, 0:255], pk[0:64, 0:255], [pkb], [yb])
                t, tb = gtmp[0]
                t2, t2b = gtmp[1]
                kb.tt("dve", t[0:64, 0:255], y[0:64, 0:255], cmpc[:, 0:255], ALU.mult, [yb, cmpcb], [tb])
                kb.tt("pool", t2[0:32, 0:255], y[32:64, 0:255], cmps[32:64, 0:255], ALU.mult, [yb, cmpsb], [t2b])
                kb.tt("pool", t2[32:64, 0:255], y[0:32, 0:255], cmps[0:32, 0:255], ALU.mult, [yb, cmpsb], [t2b])
                kb.tt("dve", kcmp[0:64, 0:255], t[0:64, 0:255], t2[0:64, 0:255], ALU.add, [tb, t2b], [kcmpb])
                kb.cp("pool", kcmp[64:128, 0:255], kcmp[0:64, 0:255], [kcmpb], [kcmpb])
            else:
                for nt in range(2):
                    nn = 128 if nt == 0 else 127
                    pv, pvb = kb.psum()
                    kb.mm(pv[0:nn, 0:64], hb[:, nt * 128:nt * 128 + nn], w2t[:], True, True, [hbb, w2b_], [pvb])
                    kb.cp("act", vcmp[0:nn, nt, 0:64], pv[0:nn, 0:64], [pvb], [vcmpb])
        for qt in range(8):
            t0 = qt * 512
            q_, qb_ = qt_s[qi % 2]
            qi += 1
            for cc2 in range(2):
                kb.dma(q_[:, cc2, :], sc["qT"].ap()[(g * 2 + cc2) * 128:(g * 2 + cc2 + 1) * 128, t0:t0 + 512], qb_, (), [qb_])
            nnt = 2 if qt >= 4 else 1
            gl32, gl32b = gl_s[qi % 2]
            kb.dma(gl32[:], sc["glT"].ap()[0:48, t0:t0 + 512], gl32b, (), [gl32b])
            glh, glhb = glh_s[qi % 2]
            gll, gllb = gll_s[qi % 2]
            kb.cp("pool", glh[:], gl32[:], [gl32b], [glhb])
            kb.tt("pool", gld[:], gl32[:], glh[:], ALU.subtract, [gl32b, glhb], [gldb])
            kb.cp("pool", gll[:], gld[:], [gldb], [gllb])
            glt = (glh, glhb, gll, gllb)
            keep, keepb = keep_s[qi % 2]
            addm, addmb = addm_s[qi % 2]
            kb.dma(keep[:].rearrange("p a b -> p (a b)"), C["keep"].ap()[:, qt * 256:(qt + 1) * 256], keepb, (), [keepb])
            kb.dma(addm[:].rearrange("p a b -> p (a b)"), C["addm"].ap()[:, qt * 256:(qt + 1) * 256], addmb, (), [addmb])
            for r in range(4):
                half = r % 2
                q_ap = q_[half * 64:(half + 1) * 64, r // 2, :]
                po, pob = next_po()
                es = []
                for nt in range(nnt):
                    e, eb = attend(kcmp, kcmpb, half, nt * 128, vcmp[:, nt, :], vcmpb, cmT[:, nt, t0:t0 + 512], cmTb,
                                   q_ap, qb_, po, pob, nt == 0, nt == nnt - 1, nt)
                    es.append((e, eb))

                def imp_block(es=es, r=r, nnt=nnt):
                    for sub in range(4):
                        pi_, pib = kb.psum()
                        for nt in range(nnt):
                            e, eb = es[nt]
                            kb.mm(pi_[:, 0:65], e[:, sub * 128:(sub + 1) * 128], ovb[:, nt, :], nt == 0, nt == nnt - 1, [eb, ovbb], [pib])
                        kb.ts("dve", rden[:], pi_[:, 64:65], 1e-30, None, ALU.max, None, [pib], [rdenb])
                        kb.recip(rden[:], rden[:], [rdenb], [rdenb])
                        if r == 0:
                            kb.ts("dve", impa[:, sub, :], pi_[:, 0:64], rden[:, 0:1], None, ALU.mult, None, [pib, rdenb], [impab])
                        else:
                            kb.stt(impa[:, sub, :], pi_[:, 0:64], rden[:, 0:1], impa[:, sub, :], ALU.mult, ALU.add, [pib, rdenb, impab], [impab])
                p_add(LA, imp_block)
                p_tick()
                finish_branch(po, pob, (g * 4 + r) * 3 + 0, True, glt, oaccs[r][0], oaccs[r][1])
            def topk_chain(keep=keep, keepb=keepb, addm=addm, addmb=addmb):
                for sub in range(4):
                    kb.tt("dve", imps[:], impa[:, sub, :], keep[:, sub, :], ALU.mult, [impab, keepb], [impsb])
                    kb.tt("dve", imps[:], imps[:], addm[:, sub, :], ALU.add, [impsb, addmb], [impsb])
                    P.op("dve", lambda e: e.max(out=m8[:], in_=imps[:]), [impsb], [m8b])
                    P.op("dve", lambda e: e.tensor_reduce(out=thr[:], in_=m8[:], axis=AX.X, op=ALU.min), [m8b], [thrb])
                    kb.ts("dve", imp2[:], imps[:], thr[:, 0:1], 1e6, ALU.is_ge, ALU.mult, [impsb, thrb], [imp2b])
                    kb.tt("dve", imp2[:], imps[:], imp2[:], ALU.subtract, [impsb, imp2b], [imp2b])
                    P.op("dve", lambda e: e.max(out=m8[:], in_=imp2[:]), [imp2b], [m8b])
                    P.op("dve", lambda e: e.tensor_reduce(out=thr[:], in_=m8[:], axis=AX.X, op=ALU.min), [m8b], [thrb])
                    kb.ts("dve", sel4[:, sub, :], imps[:], thr[:, 0:1], None, ALU.is_ge, None, [impsb, thrb], [sel4b])
            p_add(LA + 1, topk_chain)
            p_tick()
            nkc = 4 * qt + 4
            for r in range(4):
                half = r % 2
                hh = g * 4 + r
                q_ap = q_[half * 64:(half + 1) * 64, r // 2, :]
                po, pob = next_po()
                lo = max(0, 4 * qt - 4)
                for kc_ in range(lo, nkc):
                    rel = kc_ - 4 * qt
                    attend(kw, kwb, half, kc_ * 128, vw[:, kc_, :], vwb, wm[:, rel + 4, :], wmb, q_ap, qb_, po, pob,
                           kc_ == lo, kc_ == nkc - 1, kc_)
                finish_branch(po, pob, hh * 3 + 2, False, glt, oaccs[r][0], oaccs[r][1])
            p_flush()
            for sub in range(4):
                pt, ptb = kb.psum()
                kb.tr(pt[0:64, 0:128], sel4[:, sub, :], ident[:], [sel4b, identb], [ptb])
                kb.cp("act", selT[:, sub * 128:(sub + 1) * 128], pt[0:64, 0:128], [ptb], [selTb])
            for kc_ in range(nkc):
                pm, pmb = kb.psum()
                kb.mm(pm[:], eall[:, kc_ * 128:(kc_ + 1) * 128], selT[:], True, True, [eallb, selTb], [pmb])
                rel = kc_ - 4 * qt
                if rel >= 0:
                    kb.tt("dve", selm[:, kc_, :], pm[:], wm[:, rel + 4, :], ALU.mult, [pmb, wmb], [selmb])
                else:
                    kb.cp("act", selm[:, kc_, :], pm[:], [pmb], [selmb])
            for r in range(4):
                half = r % 2
                hh = g * 4 + r
                q_ap = q_[half * 64:(half + 1) * 64, r // 2, :]
                oacc, oaccb = oaccs[r]
                po, pob = next_po()
                for kc_ in range(nkc):
                    attend(ks, ksb, half, kc_ * 128, vs[:, kc_, :], vsb, selm[:, kc_, :], selmb, q_ap, qb_, po, pob,
                           kc_ == 0, kc_ == nkc - 1, kc_)
                finish_branch(po, pob, hh * 3 + 1, False, glt, oacc, oaccb)

                def out_fn(r=r, half=half, oacc=oacc, oaccb=oaccb, g=g, t0=t0):
                    ob_, obb_ = obuf[(r // 2) % 2]
                    kb.cp("act", ob_[half * 64:(half + 1) * 64, :], oacc[:], [oaccb], [obb_])
                    if half == 1:
                        cch = g * 2 + r // 2
                        kb.dma(sc["ocT"].ap()[cch * 128:(cch + 1) * 128, t0:t0 + 512], ob_[:], obb_, [obb_], ())
                p_add(LA + 3, out_fn)
                p_tick()
        p_flush()
    P.barrier()
    A.release(mk)
    kb.nrot = 8


def phase_merge(E, l, dst):
    kb, A, P = E["kb"], E["kb"].A, E["kb"].P
    d = E["lay"][l]
    sc = E["scr"]
    ones, onesb = E["ones"]
    mo, mob = E["mods"][l]
    m1, m1b = E["mod1"][l]
    (lg, lgb), (lb_, lbb) = E["lnp"][l]
    hT = E["hT"]
    mk = A.mark()
    kb.nrot = 8

    def ld(name, shape, src):
        t, b = kb.sb(shape, F32, name, dma=True)
        kb.dma(t[:], src.ap(), b, (), [b])
        return t, b
    nag, nagb = ld("nag", [128, 8], d["normag"])
    nab, nabb = ld("nab", [128, 8], d["normab"])
    snw, snwb = ld("snw", [128, 8], d["ssmnw"])
    W = {}
    for nm, nch in (("waout", 8), ("wbout", 8), ("wcout", 8), ("wo", 8), ("wgate", 24)):
        W[nm] = kb.sb([128, nch, 1024], BF16, nm)
    mk2 = A.mark()
    stg = [kb.sb([128, 1024], F32, "stg", dma=True) for _ in range(3)]
    i = 0
    for nm, nch in (("waout", 8), ("wbout", 8), ("wcout", 8), ("wo", 8), ("wgate", 24)):
        t, b = W[nm]
        for c in range(nch):
            st, stb = stg[i % 3]
            kb.dma(st[:], d[nm].ap()[c], stb, (), [stb])
            kb.cp("pool" if i % 2 == 0 else "dve", t[:, c, :], st[:], [stb], [b])
            i += 1
    P.barrier()
    A.release(mk2)
    h, hb = kb.sb([128, 8, 512], F32, "h", dma=True)
    u, ub = kb.sb([128, 8, 512], BF16, "u")
    raw, rawb = kb.sb([128, 8, 512], F32, "raw", dma=True)
    aA, aAb = kb.sb([128, 8, 512], BF16, "aA")
    aB, aBb = kb.sb([128, 8, 512], BF16, "aB")
    aC, aCb = kb.sb([128, 8, 512], BF16, "aC", dma=True)
    mg, mgb = kb.sb([128, 8, 512], BF16, "mg")
    sig = [kb.sb([128, 512], F32, "sig") for _ in range(2)]
    msum, msumb = kb.sb([128, 512], F32, "msum")
    mt, mtb = kb.sb([128, 512], F32, "mt")
    Cn = {"mean": kb.sb([128, 512], F32, "mean"), "rstd": kb.sb([128, 512], F32, "rstd"),
          "sq": kb.sb([128, 512], F32, "sq"), "epscol": E["epsc"]}
    mean, meanb = Cn["mean"]
    rstd, rstdb = Cn["rstd"]
    si = 0
    for t in range(8):
        tsl = slice(t * 512, (t + 1) * 512)
        for k in range(8):
            kb.dma(h[:, k, :], hT.ap()[k * 128:(k + 1) * 128, tsl], hb, (), [hb])
        for k in range(8):
            kb.act(u[:, k, :], h[:, k, :], AF.Identity, [hb, mob, m1b], [ub],
                   bias=mo[:, 24 + k:24 + k + 1], scale=m1[:, 32 + k:32 + k + 1])
        for k in range(8):
            kb.dma(raw[:, k, :], sc["aconvT"].ap()[k * 128:(k + 1) * 128, tsl], rawb, (), [rawb])
        colnorm_stats(kb, Cn, lambda k: (raw[:, k, :], rawb), 8, 512, ones, onesb, Cn["sq"][0], Cn["sq"][1], LN_EPS)
        for k in range(8):
            eng = "dve" if k % 2 == 0 else "pool"
            kb.tt(eng, raw[:, k, :], raw[:, k, :], mean[:], ALU.subtract, [rawb, meanb], [rawb])
            kb.tt(eng, raw[:, k, :], raw[:, k, :], rstd[:], ALU.mult, [rawb, rstdb], [rawb])
            kb.act(raw[:, k, :], raw[:, k, :], AF.Identity, [rawb, nagb, nabb], [rawb], bias=nab[:, k:k + 1], scale=nag[:, k:k + 1])
            kb.act(aA[:, k, :], raw[:, k, :], AF.Silu, [rawb], [aAb])
        for k in range(8):
            kb.dma(raw[:, k, :], sc["ybT"].ap()[k * 128:(k + 1) * 128, tsl], rawb, (), [rawb])
        colnorm_stats(kb, Cn, lambda k: (raw[:, k, :], rawb), 8, 512, ones, onesb, Cn["sq"][0], Cn["sq"][1], LN_EPS, center=False)
        for k in range(8):
            eng = "dve" if k % 2 == 0 else "pool"
            kb.tt(eng, raw[:, k, :], raw[:, k, :], rstd[:], ALU.mult, [rawb, rstdb], [rawb])
            kb.act(aB[:, k, :], raw[:, k, :], AF.Identity, [rawb, snwb], [aBb], scale=snw[:, k:k + 1])
        for k in range(8):
            kb.dma(aC[:, k, :], sc["ocT"].ap()[k * 128:(k + 1) * 128, tsl], aCb, (), [aCb])
        for c in range(8):
            for bi, (wn, act_, actb) in enumerate((("waout", aA, aAb), ("wbout", aB, aBb), ("wcout", aC, aCb))):
                wt, wtb = W[wn]
                wg_, wgb_ = W["wgate"]
                py, pyb = kb.psum()
                pg, pgb = kb.psum()
                for k in range(8):
                    kb.mm(py[:], wt[:, c, k * 128:(k + 1) * 128], act_[:, k, :], k == 0, k == 7, [wtb, actb], [pyb])
                for k in range(8):
                    kb.mm(pg[:], wg_[:, bi * 8 + c, k * 128:(k + 1) * 128], u[:, k, :], k == 0, k == 7, [wgb_, ub], [pgb])
                s_, s_b = sig[si % 2]
                si += 1
                kb.act(s_[:], pg[:], AF.Sigmoid, [pgb], [s_b])
                if bi == 0:
                    kb.tt("dve", msum[:], s_[:], py[:], ALU.mult, [s_b, pyb], [msumb])
                elif bi == 1:
                    kb.tt("dve", mt[:], s_[:], py[:], ALU.mult, [s_b, pyb], [mtb])
                    kb.tt("pool", msum[:], msum[:], mt[:], ALU.add, [msumb, mtb], [msumb])
                else:
                    kb.tt("dve", mt[:], s_[:], py[:], ALU.mult, [s_b, pyb], [mtb])
                    kb.tt("pool", mg[:, c, :], msum[:], mt[:], ALU.add, [msumb, mtb], [mgb])
        wo, wob = W["wo"]
        for c in range(8):
            py, pyb = kb.psum()
            for k in range(8):
                kb.mm(py[:], wo[:, c, k * 128:(k + 1) * 128], mg[:, k, :], k == 0, k == 7, [wob, mgb], [pyb])
            s_, s_b = sig[si % 2]
            si += 1
            kb.act(s_[:], py[:], AF.Identity, [pyb, m1b], [s_b], scale=m1[:, 40 + c:40 + c + 1])
            kb.stt(h[:, c, :], h[:, c, :], DN_ALPHA, s_[:], ALU.mult, ALU.add, [hb, s_b], [hb])
        colnorm_stats(kb, Cn, lambda k: (h[:, k, :], hb), 8, 512, ones, onesb, Cn["sq"][0], Cn["sq"][1], LN_EPS)
        for k in range(8):
            eng = "dve" if k % 2 == 0 else "pool"
            kb.tt(eng, h[:, k, :], h[:, k, :], mean[:], ALU.subtract, [hb, meanb], [hb])
            kb.tt(eng, h[:, k, :], h[:, k, :], rstd[:], ALU.mult, [hb, rstdb], [hb])
            kb.act(h[:, k, :], h[:, k, :], AF.Identity, [hb, lgb, lbb], [hb], bias=lb_[:, 8 + k:8 + k + 1], scale=lg[:, 8 + k:8 + k + 1])
        for k in range(8):
            kb.dma(dst.ap()[k * 128:(k + 1) * 128, tsl], h[:, k, :], hb, [hb], ())
    P.barrier()
    A.release(mk)


def build(nlayers=L, stop_after=None, debug=False, mixsub=None):
    nc = bass.Bass("TRN2", target_bir_lowering=False)
    kb = KB(nc)
    P = kb.P
    A = kb.A
    xT = kb.inp("xT", [D, S])
    ccol = kb.inp("ccol", [128, 8])
    outT = nc.dram_tensor("outT", [D, S], F32, kind="ExternalOutput")
    hT = kb.scratch("hT", [D, S])
    hTb = [kb.buf("hT") for _ in range(4)]
    lay = []
    for l in range(nlayers):
        d = {}
        d["adaw"] = kb.inp("adaw%d" % l, [72, 128, 1024])
        d["adab"] = kb.inp("adab%d" % l, [128, 72])
        d["lng"] = kb.inp("lng%d" % l, [128, 24])
        d["lnb"] = kb.inp("lnb%d" % l, [128, 24])
        for i in range(2):
            d["wg%d" % i] = kb.inp("wg%d_%d" % (l, i), [22, 128, 1024])
            d["wu%d" % i] = kb.inp("wu%d_%d" % (l, i), [22, 128, 1024])
            d["wd%d" % i] = kb.inp("wd%d_%d" % (l, i), [8, 128, 22 * 128])
        for nm, shp in MIX_SHAPES.items():
            d[nm] = kb.inp("%s%d" % (nm, l), shp)
        lay.append(d)
    consts = {nm: kb.inp("c_" + nm, shp) for nm, shp in CONST_SHAPES.items()}
    scr = {"aconvT": kb.scratch("aconvT", [D, S]), "zsT": kb.scratch("zsT", [D, S]), "xsT": kb.scratch("xsT", [D, S]),
           "BT": kb.scratch("BT", [512, S]), "CT": kb.scratch("CT", [512, S]), "glT": kb.scratch("glT", [128, S]),
           "qT": kb.scratch("qT", [D, S], BF16), "kcT": kb.scratch("kcT", [256, S], BF16), "vcT": kb.scratch("vcT", [256, S], BF16),
           "ksT": kb.scratch("ksT", [256, S], BF16), "kwT": kb.scratch("kwT", [256, S], BF16),
           "vsw": kb.scratch("vsw", [S, 512], BF16), "ybT": kb.scratch("ybT", [D, S]), "ocT": kb.scratch("ocT", [D, S], BF16)}
    if debug:
        scr["ybT"] = nc.dram_tensor("dbg_ybT", [D, S], F32, kind="ExternalOutput")
        scr["aconvT"] = nc.dram_tensor("dbg_aconvT", [D, S], F32, kind="ExternalOutput")
        scr["ocT"] = nc.dram_tensor("dbg_ocT", [D, S], BF16, kind="ExternalOutput")

    ones, onesb = kb.sb([128, 128], F32, "ones")
    kb.memset("dve", ones[:], 1.0, [onesb])
    epsc, epscb = kb.sb([128, 1], F32, "eps")
    kb.memset("dve", epsc[:], LN_EPS, [epscb])
    cc, ccb = kb.sb([128, 8], F32, "cc", dma=True)
    kb.dma(cc[:], ccol.ap(), ccb, (), [ccb])
    sc, scb = kb.sb([128, 8], F32, "sc")
    kb.act(sc[:], cc[:], AF.Silu, [ccb], [scb])
    mods = [kb.sb([128, 72], F32, "mod") for _ in range(nlayers)]
    mod1 = [kb.sb([128, 72], F32, "mod1") for _ in range(nlayers)]
    lnp = [(kb.sb([128, 24], F32, "lng", dma=True), kb.sb([128, 24], F32, "lnb", dma=True)) for _ in range(nlayers)]
    dt_pers = kb.sb([128, 32, 16], F32, "dtp")
    E = {"kb": kb, "lay": lay, "scr": scr, "consts": consts, "mods": mods, "mod1": mod1, "lnp": lnp, "hT": hT,
         "ones": (ones, onesb), "epsc": (epsc, epscb), "dt": dt_pers}
    base_mark = A.mark()

    def phase_ada(l):
        m = A.mark()
        d = lay[l]
        ab, abb = kb.sb([128, 72], F32, "adab", dma=True)
        kb.dma(ab[:], d["adab"].ap(), abb, (), [abb])
        (g, gb), (b_, bb) = lnp[l]
        kb.dma(g[:], d["lng"].ap(), gb, (), [gb])
        kb.dma(b_[:], d["lnb"].ap(), bb, (), [bb])
        slots = [kb.sb([128, 1024], F32, "adaw", dma=True) for _ in range(3)]
        ps, psb = kb.psum()
        for j in range(72):
            w, wb = slots[j % 3]
            kb.dma(w[:], d["adaw"].ap()[j], wb, (), [wb])
            for k in range(8):
                kb.mm(ps[:, j:j + 1], w[:, k * 128:(k + 1) * 128], sc[:, k:k + 1], k == 0, k == 7, [wb, scb], [psb])
        mo, mob = mods[l]
        kb.tt("dve", mo[:], ps[:, 0:72], ab[:], ALU.add, [psb, abb], [mob])
        m1, m1b = mod1[l]
        kb.cp("dve", m1[:], mo[:], [mob], [m1b])
        for sub in range(3):
            c0 = (sub * 3 + 1) * 8
            kb.ts("dve", m1[:, c0:c0 + 8], mo[:, c0:c0 + 8], 1.0, None, ALU.add, None, [mob], [m1b])
            c0 = (sub * 3 + 2) * 8
            if sub != 1:
                kb.ts("dve", m1[:, c0:c0 + 8], mo[:, c0:c0 + 8], 0.5, None, ALU.mult, None, [mob], [m1b])
        P.barrier()
        A.release(m)

    def ln_apply(l, i, r, rb, hsrc_cols, NTOK, C):
        (g, gb), (b_, bb) = lnp[l]
        for t0 in range(0, NTOK, 512):
            colnorm_stats(kb, C, lambda k: (r[:, k, t0:t0 + 512], rb), 8, 512, ones, onesb, C["sq"][0], C["sq"][1], LN_EPS)
            mean, meanb = C["mean"]
            rstd, rstdb = C["rstd"]
            for k in range(8):
                eng = "dve" if k % 2 == 0 else "pool"
                kb.tt(eng, r[:, k, t0:t0 + 512], r[:, k, t0:t0 + 512], mean[:, 0:512], ALU.subtract, [rb, meanb], [rb])
                kb.tt(eng, r[:, k, t0:t0 + 512], r[:, k, t0:t0 + 512], rstd[:, 0:512], ALU.mult, [rb, rstdb], [rb])
                kb.act(r[:, k, t0:t0 + 512], r[:, k, t0:t0 + 512], AF.Identity, [rb, gb, bb], [rb],
                       bias=b_[:, i * 8 + k:i * 8 + k + 1], scale=g[:, i * 8 + k:i * 8 + k + 1])

    def phase_ffn(l, i, src, dst):
        sub = 0 if i == 0 else 2
        d = lay[l]
        m1, m1b = mod1[l]
        mo, mob = mods[l]
        m = A.mark()
        NT = 1024
        h, hb = kb.sb([128, 8, NT], F32, "h", dma=True)
        u, ub = kb.sb([128, 8, NT], BF16, "u")
        a, ab = kb.sb([128, 22, NT], BF16, "a")
        sg, sgb = kb.sb([128, 512], F32, "sg")
        sg2, sg2b = kb.sb([128, 512], F32, "sg2")
        C = {"mean": kb.sb([128, 512], F32, "mean"), "rstd": kb.sb([128, 512], F32, "rstd"),
             "sq": kb.sb([128, 512], F32, "sq"), "epscol": (epsc, epscb)}
        wgs = WStream(kb, 1024, 2, "wg")
        wus = WStream(kb, 1024, 2, "wu")
        wds = WStream(kb, 22 * 128, 2, "wd")
        for qt in range(S // NT):
            tsl = slice(qt * NT, (qt + 1) * NT)
            for k in range(8):
                kb.dma(h[:, k, :], src.ap()[k * 128:(k + 1) * 128, tsl], hb, [hTb[qt]], [hb])
            wgs.prefetch(d["wg%d" % i].ap()[0])
            wus.prefetch(d["wu%d" % i].ap()[0])
            for k in range(8):
                kb.act(u[:, k, :], h[:, k, :], AF.Identity, [hb, mob, m1b], [ub],
                       bias=mo[:, (sub * 3) * 8 + k:(sub * 3) * 8 + k + 1],
                       scale=m1[:, (sub * 3 + 1) * 8 + k:(sub * 3 + 1) * 8 + k + 1])
            for f in range(22):
                if f + 1 < 22:
                    wgs.prefetch(d["wg%d" % i].ap()[f + 1])
                    wus.prefetch(d["wu%d" % i].ap()[f + 1])
                else:
                    wds.prefetch(d["wd%d" % i].ap()[0])
                wg, wgb = wgs.get("pool")
                wu, wub = wus.get("pool")
                nt_ = NT // 512
                pgs = [kb.psum() for _ in range(nt_)]
                pus = [kb.psum() for _ in range(nt_)]
                for k in range(8):
                    for t in range(nt_):
                        kb.mm(pgs[t][0][:], wg[:, k * 128:(k + 1) * 128], u[:, k, t * 512:(t + 1) * 512], k == 0, k == 7, [wgb, ub], [pgs[t][1]])
                for k in range(8):
                    for t in range(nt_):
                        kb.mm(pus[t][0][:], wu[:, k * 128:(k + 1) * 128], u[:, k, t * 512:(t + 1) * 512], k == 0, k == 7, [wub, ub], [pus[t][1]])
                for t in range(nt_):
                    t0 = t * 512
                    s_, s_b = (sg, sgb) if t % 2 == 0 else (sg2, sg2b)
                    kb.act(s_[:], pgs[t][0][:], AF.Silu, [pgs[t][1]], [s_b])
                    kb.tt("dve", a[:, f, t0:t0 + 512], s_[:], pus[t][0][:], ALU.mult, [s_b, pus[t][1]], [ab])
            for dd in range(8):
                if dd + 1 < 8:
                    wds.prefetch(d["wd%d" % i].ap()[dd + 1])
                wd, wdb = wds.get("pool")
                for t in range(NT // 512):
                    t0 = t * 512
                    py, pyb = kb.psum()
                    for f in range(22):
                        kb.mm(py[:], wd[:, f * 128:(f + 1) * 128], a[:, f, t0:t0 + 512], f == 0, f == 21, [wdb, ab], [pyb])
                    s_, s_b = (sg, sgb) if t % 2 == 0 else (sg2, sg2b)
                    kb.act(s_[:], py[:], AF.Identity, [pyb, m1b], [s_b], scale=m1[:, (sub * 3 + 2) * 8 + dd:(sub * 3 + 2) * 8 + dd + 1])
                    kb.stt(h[:, dd, t0:t0 + 512], h[:, dd, t0:t0 + 512], DN_ALPHA, s_[:], ALU.mult, ALU.add, [hb, s_b], [hb])
            ln_apply(l, i if i == 0 else 2, h, hb, None, NT, C)
            for k in range(8):
                kb.dma(dst.ap()[k * 128:(k + 1) * 128, tsl], h[:, k, :], hb, [hb], [hTb[qt]])
        P.barrier()
        A.release(m)

    phases = []
    for l in range(nlayers):
        phases.append(("ada", l))
        phases.append(("ffn0", l))
        phases.append(("mix", l))
        phases.append(("ffn1", l))
    if stop_after is not None:
        phases = phases[:stop_after]
    cur = xT
    n = len(phases)
    for pi, (ph, l) in enumerate(phases):
        last = pi == n - 1
        if ph == "ada":
            phase_ada(l)
        elif ph == "ffn0":
            phase_ffn(l, 0, cur, outT if last else hT)
            cur = hT
        elif ph == "ffn1":
            phase_ffn(l, 1, cur, outT if last else hT)
            cur = hT
        elif ph == "mix":
            sub = mixsub or ("m1", "ssd", "nsa", "merge")
            if "m1" in sub:
                phase_m1(E, l)
            if "ssd" in sub:
                phase_ssd(E, l)
            if "nsa" in sub:
                phase_nsa(E, l)
            if "merge" in sub:
                phase_merge(E, l, outT if last else hT)
            cur = hT
    P.emit()
    return nc


def chunkW(W):
    Kt, N = W.shape
    return np.ascontiguousarray(W.reshape(Kt // 128, 128, N // 128, 128).transpose(2, 1, 0, 3)).reshape(N // 128, 128, Kt)


def colvec(v):
    return np.ascontiguousarray(v.reshape(-1, 128).T)


def prep_inputs(inputs, b, nlayers=L):
    f = np.float32
    m = {}
    m["xT"] = np.ascontiguousarray(inputs["x"][b].T)
    m["ccol"] = colvec(inputs["c"][b])
    for l in range(nlayers):
        m["adaw%d" % l] = chunkW(inputs["ada_w"][l])
        m["adab%d" % l] = colvec(inputs["ada_b"][l])
        m["lng%d" % l] = colvec(inputs["ln_g"][l].reshape(-1))
        m["lnb%d" % l] = colvec(inputs["ln_b"][l].reshape(-1))
        for i in range(2):
            m["wg%d_%d" % (l, i)] = chunkW(inputs["ffn_w_gate"][l, i])
            m["wu%d_%d" % (l, i)] = chunkW(inputs["ffn_w_up"][l, i])
            m["wd%d_%d" % (l, i)] = chunkW(inputs["ffn_w_down"][l, i])
        prep_mixer_inputs(inputs, l, m)
    for k_, v_ in host_consts().items():
        m["c_" + k_] = v_
    return {k: np.ascontiguousarray(v, dtype=f) for k, v in m.items()}


_NC_CACHE = {}


def kernel(**inputs):
    inputs = {k: np.asarray(v) for k, v in inputs.items()}
    if "nc" not in _NC_CACHE:
        _NC_CACHE["nc"] = build()
    nc = _NC_CACHE["nc"]
    in_maps = [prep_inputs(inputs, b) for b in range(NCORES)]
    res = run_bass_kernel_spmd(nc, in_maps, core_ids=list(range(NCORES)))
    out = np.stack([np.ascontiguousarray(res.results[b]["outT"].T) for b in range(NCORES)], 0)
    return out.astype(np.float32)
```

```python
import math
import numpy as np
import concourse.bass as bass
import concourse.mybir as mybir
from concourse.bass_utils import run_bass_kernel_spmd
from contextlib import ExitStack

F32 = mybir.dt.float32
BF16 = mybir.dt.bfloat16
AF = mybir.ActivationFunctionType
ALU = mybir.AluOpType
AX = mybir.AxisListType

EPOCH = 30000
SSDVAR = 2

D = 1024
S = 4096
DFF = 2816
L = 2
DN_ALPHA = (2.0 * L) ** 0.25
LN_EPS = 1e-5
NCORES = 4


class Buf:
    __slots__ = ("name", "w", "r", "sem", "dval")

    def __init__(self, name):
        self.name = name
        self.w = None
        self.r = {}
        self.sem = None
        self.dval = 0


class Prog:
    ENG = ["pe", "dve", "act", "pool", "sp"]

    def __init__(self, nc):
        self.nc = nc
        self.streams = {e: [] for e in self.ENG}
        self.cnt = {e: 0 for e in self.ENG}
        self.known = {e: {} for e in self.ENG}
        self.semkeys = {}
        self.pending = {e: {} for e in self.ENG}
        self.dma_bufs = []
        self.free_sems = []
        self.nsem = 0

    def _deps(self, reads, writes):
        deps = {}
        for b in reads:
            t = b.w
            if t is not None and deps.get(t[0], 0) < t[1]:
                deps[t[0]] = t[1]
        for b in writes:
            t = b.w
            if t is not None and deps.get(t[0], 0) < t[1]:
                deps[t[0]] = t[1]
            for k, v in b.r.items():
                if deps.get(k, 0) < v:
                    deps[k] = v
        return deps

    def _commit(self, tok, reads, writes):
        k, v = tok
        for b in reads:
            if b.r.get(k, 0) < v:
                b.r[k] = v
        for b in writes:
            b.w = tok
            b.r = {}

    def _filter(self, eng, deps):
        pend = self.pending[eng]
        if pend:
            for k, v in pend.items():
                if deps.get(k, 0) < v:
                    deps[k] = v
            self.pending[eng] = {}
        waits = []
        kn = self.known[eng]
        for k, v in deps.items():
            if eng == "pe" and k.startswith("Epe"):
                continue
            if kn.get(k, 0) >= v:
                continue
            kn[k] = v
            waits.append((k, v))
            self.semkeys[k] = True
        return waits

    def op(self, eng, fn, reads=(), writes=()):
        deps = self._deps(reads, writes)
        waits = self._filter(eng, deps)
        c = self.cnt[eng]
        self.cnt[eng] = c + 1
        key = "E%s%d" % (eng, c // EPOCH)
        tok = (key, c % EPOCH + 1)
        self.semkeys[key] = True
        self.streams[eng].append((waits, fn, key, 1))
        self._commit(tok, reads, writes)
        return tok

    def dma(self, out, in_, sb, reads=(), writes=(), q="sp"):
        deps = self._deps(reads, writes)
        waits = self._filter(q, deps)
        if sb.sem is not None and sb.dval + 16 > EPOCH:
            sb.sem = None
        if sb.sem is None:
            while self.free_sems:
                nm, val = self.free_sems.pop()
                if val + 4096 <= EPOCH:
                    sb.sem, sb.dval = nm, val
                    break
            if sb.sem is None:
                self.nsem += 1
                sb.sem, sb.dval = "D%d" % self.nsem, 0
            self.dma_bufs.append(sb)
        sb.dval += 16
        key = sb.sem
        tok = (key, sb.dval)
        self.semkeys[key] = True
        self.streams[q].append((waits, lambda e: e.dma_start(out=out, in_=in_), key, 16))
        self._commit(tok, reads, writes)
        return tok

    def barrier(self):
        allk = {}
        for e in self.ENG:
            c = self.cnt[e]
            if c:
                allk["E%s%d" % (e, (c - 1) // EPOCH)] = (c - 1) % EPOCH + 1
        for b in self.dma_bufs:
            if b.sem is not None:
                if allk.get(b.sem, 0) < b.dval:
                    allk[b.sem] = b.dval
                self.free_sems.append((b.sem, b.dval))
                b.sem = None
        self.dma_bufs = []
        for e in self.ENG:
            p = self.pending[e]
            for k, v in allk.items():
                if p.get(k, 0) < v:
                    p[k] = v

    def dbuf(self, name):
        return Buf(name)

    def emit(self):
        nc = self.nc
        self.barrier()
        for e in self.ENG:
            waits = self._filter(e, {})
            self.streams[e].append((waits, None, None, 0))
        with ExitStack() as st:
            sems = {}
            for k in self.semkeys:
                sems[k] = st.enter_context(nc.semaphore(k))
            block = st.enter_context(nc.Block())

            def run(e, stream):
                for (waits, fn, key, inc) in stream:
                    for (k, v) in waits:
                        e.wait_ge(sems[k], v)
                    if fn is not None:
                        fn(e).then_inc(sems[key], inc)

            @block.tensor
            def _(e):
                run(e, self.streams["pe"])

            @block.vector
            def _(e):
                run(e, self.streams["dve"])

            @block.scalar
            def _(e):
                run(e, self.streams["act"])

            @block.gpsimd
            def _(e):
                run(e, self.streams["pool"])

            @block.sync
            def _(e):
                run(e, self.streams["sp"])


class Arena:
    def __init__(self, nc):
        self.nc = nc
        self.off = 16512
        self.limit = 229376
        self.n = 0

    def alloc(self, shape, dtype, name="t"):
        nbytes = int(np.prod(shape[1:])) * (4 if dtype == F32 else 2)
        nbytes = (nbytes + 63) // 64 * 64
        self.n += 1
        t = self.nc.alloc_sbuf_tensor_at("%s_%d" % (name, self.n), list(shape), dtype, offset=self.off)
        self.off += nbytes
        assert self.off <= self.limit, "SBUF arena overflow %d" % self.off
        return t

    def mark(self):
        return self.off

    def release(self, m):
        self.off = m


class KB:
    def __init__(self, nc):
        self.nc = nc
        self.P = Prog(nc)
        self.A = Arena(nc)
        self.ps = [nc.alloc_psum_tensor("ps%d" % i, [128, 512], F32) for i in range(8)]
        self.psb = [Buf("ps%d" % i) for i in range(8)]
        self.psi = 0
        self.nrot = 8
        self.din = {}
        self.nb = 0

    def inp(self, name, shape, dtype=F32):
        t = self.nc.dram_tensor(name, list(shape), dtype, kind="ExternalInput")
        self.din[name] = t
        return t

    def scratch(self, name, shape, dtype=F32):
        return self.nc.dram_tensor(name, list(shape), dtype, kind="Internal")

    def buf(self, name="b"):
        self.nb += 1
        return Buf("%s%d" % (name, self.nb))

    def dbuf(self, name="d"):
        self.nb += 1
        return self.P.dbuf("%s%d" % (name, self.nb))

    def sb(self, shape, dtype=F32, name="t", dma=False):
        t = self.A.alloc(shape, dtype, name)
        b = self.dbuf(name) if dma else self.buf(name)
        return t, b

    def psum(self):
        i = self.psi % self.nrot
        self.psi = (i + 1) % self.nrot
        return self.ps[i], self.psb[i]

    def mm(self, out, lhsT, rhs, st, sp, R, W):
        self.P.op("pe", lambda e: e.matmul(out, lhsT=lhsT, rhs=rhs, start=st, stop=sp), R, W)

    def tr(self, out, in_, ident, R, W):
        self.P.op("pe", lambda e: e.transpose(out, in_, ident), R, W)

    def act(self, out, in_, func, R, W, bias=0.0, scale=1.0):
        self.P.op("act", lambda e: e.activation(out=out, in_=in_, func=func, bias=bias, scale=scale), R, W)

    def tt(self, eng, out, in0, in1, op, R, W):
        self.P.op(eng, lambda e: e.tensor_tensor(out=out, in0=in0, in1=in1, op=op), R, W)

    def ts(self, eng, out, in0, s1, s2, op0, op1, R, W):
        if s2 is None:
            self.P.op(eng, lambda e: e.tensor_scalar(out=out, in0=in0, scalar1=s1, scalar2=None, op0=op0), R, W)
        else:
            self.P.op(eng, lambda e: e.tensor_scalar(out=out, in0=in0, scalar1=s1, scalar2=s2, op0=op0, op1=op1), R, W)

    def stt(self, out, in0, scalar, in1, op0, op1, R, W):
        self.P.op("dve", lambda e: e.scalar_tensor_tensor(out=out, in0=in0, scalar=scalar, in1=in1, op0=op0, op1=op1), R, W)

    def cp(self, eng, out, in_, R, W):
        if eng == "act":
            self.P.op("act", lambda e: e.copy(out=out, in_=in_), R, W)
        else:
            self.P.op(eng, lambda e: e.tensor_copy(out=out, in_=in_), R, W)

    def memset(self, eng, ap, val, W):
        self.P.op(eng, lambda e: e.memset(ap, val), (), W)

    def recip(self, out, in_, R, W):
        self.P.op("dve", lambda e: e.reciprocal(out=out, in_=in_), R, W)

    def dma(self, out, in_, sb, R=(), W=()):
        self.P.dma(out, in_, sb, R, W)


class WStream:
    def __init__(self, kb, n, nslots=2, name="w"):
        self.kb = kb
        self.n = n
        self.st = [kb.sb([128, n], F32, name + "s", dma=True) for _ in range(nslots)]
        self.bf = [kb.sb([128, n], BF16, name + "b") for _ in range(nslots)]
        self.i = 0
        self.ns = nslots
        self.q = []

    def prefetch(self, src, n=None):
        n = n or self.n
        kb = self.kb
        i = self.i
        self.i = (i + 1) % self.ns
        st, stb = self.st[i]
        bf, bfb = self.bf[i]
        kb.dma(st[:, 0:n], src, stb, (), [stb])
        self.q.append((i, n))

    def get(self, eng="pool"):
        kb = self.kb
        i, n = self.q.pop(0)
        st, stb = self.st[i]
        bf, bfb = self.bf[i]
        kb.cp(eng, bf[:, 0:n], st[:, 0:n], [stb], [bfb])
        return bf, bfb


def colnorm_stats(kb, C, src_fn, nk, N, ones, onesb, sq, sqb, eps, center=True):
    ps1, ps1b = kb.psum()
    ps2, ps2b = kb.psum()
    for k in range(nk):
        x, xb = src_fn(k)
        if center:
            kb.mm(ps1[:, 0:N], ones[:], x, k == 0, k == nk - 1, [onesb, xb], [ps1b])
        kb.act(sq[:, 0:N], x, AF.Square, [xb], [sqb])
        kb.mm(ps2[:, 0:N], ones[:], sq[:, 0:N], k == 0, k == nk - 1, [onesb, sqb], [ps2b])
    mean, meanb = C["mean"]
    rstd, rstdb = C["rstd"]
    inv = 1.0 / (128 * nk)
    if center:
        kb.act(mean[:, 0:N], ps1[:, 0:N], AF.Identity, [ps1b], [meanb], scale=inv)
        kb.tt("dve", rstd[:, 0:N], mean[:, 0:N], mean[:, 0:N], ALU.mult, [meanb], [rstdb])
        kb.stt(rstd[:, 0:N], ps2[:, 0:N], inv, rstd[:, 0:N], ALU.mult, ALU.subtract, [ps2b, rstdb], [rstdb])
        kb.act(rstd[:, 0:N], rstd[:, 0:N], AF.Sqrt, [rstdb], [rstdb], bias=C["epscol"][0][:, 0:1])
    else:
        kb.act(rstd[:, 0:N], ps2[:, 0:N], AF.Sqrt, [ps2b], [rstdb], bias=C["epscol"][0][:, 0:1], scale=inv)
    kb.recip(rstd[:, 0:N], rstd[:, 0:N], [rstdb], [rstdb])


def host_consts():
    c = {}
    half = 32
    inv_freq = (10000.0 ** (-np.arange(half, dtype=np.float32) / half)).astype(np.float32)
    pos = np.arange(S, dtype=np.float32)
    ang = (pos[None, :] * inv_freq[:, None]).astype(np.float32)
    cos = np.cos(ang).astype(np.float32)
    sin = np.sin(ang).astype(np.float32)
    c["ropec"] = np.tile(np.concatenate([cos, cos], 0), (2, 1))
    c["ropes"] = np.tile(np.concatenate([sin, -sin], 0), (2, 1))
    cend = (np.arange(255) * 16 + 31).astype(np.float32)
    angc = (cend[None, :] * inv_freq[:, None]).astype(np.float32)
    cc = np.zeros((64, 256), np.float32)
    ss = np.zeros((64, 256), np.float32)
    cc[:, :255] = np.concatenate([np.cos(angc)] * 2, 0)
    ss[:, :255] = np.concatenate([np.sin(angc), -np.sin(angc)], 0)
    c["cmpc"] = cc
    c["cmps"] = ss
    kk = np.arange(128)[:, None]
    tq = np.arange(512)[None, :]
    wm = np.zeros((128, 8, 512), np.float32)
    for rel in range(-4, 4):
        k = kk + 128 * rel
        wm[:, rel + 4, :] = ((k <= tq) & (k > tq - 512)).astype(np.float32)
    c["wm"] = wm.reshape(128, 8 * 512)
    n = (np.arange(2)[None, :, None] * 128 + np.arange(128)[:, None, None])
    t = np.arange(S)[None, None, :]
    c["cmaskT"] = ((16 * n + 31 <= t) & (n < 255)).astype(np.float32).reshape(128, 2 * S)
    c["eall"] = (np.arange(S)[None, :] // 64 == np.arange(64)[:, None]).astype(np.float32)
    ov = np.zeros((128, 2, 65), np.float32)
    for nt in range(2):
        for p in range(128):
            nn = nt * 128 + p
            if nn >= 255:
                continue
            for j in range(64):
                if 16 * nn < 64 * j + 64 and 16 * nn + 32 > 64 * j:
                    ov[p, nt, j] = 1.0
            ov[p, nt, 64] = 1.0
    c["ovaug"] = ov.reshape(128, 130)
    keep = np.zeros((128, 32, 64), np.float32)
    add = np.zeros((128, 32, 64), np.float32)
    for sub in range(32):
        for p in range(128):
            tt_ = sub * 128 + p
            cur = tt_ // 64
            for j in range(64):
                if j == cur:
                    add[p, sub, j] = 3e4
                elif j == cur - 1:
                    add[p, sub, j] = 2e4
                elif j == 0:
                    add[p, sub, j] = 1e4
                elif 64 * j > tt_:
                    add[p, sub, j] = -1.0 - j / 64.0
                else:
                    keep[p, sub, j] = 1.0
    c["keep"] = keep.reshape(128, 32 * 64)
    c["addm"] = add.reshape(128, 32 * 64)
    l_ = np.arange(128)[:, None, None] + 128 * np.arange(2)[None, :, None]
    c["tri"] = (l_ <= np.arange(256)[None, None, :]).astype(np.float32).reshape(128, 512)
    c["ident"] = np.eye(128, dtype=np.float32)
    return c


CONST_SHAPES = {"ropec": [128, S], "ropes": [128, S], "cmpc": [64, 256], "cmps": [64, 256], "wm": [128, 4096],
                "cmaskT": [128, 2 * S], "eall": [64, S], "ovaug": [128, 130], "keep": [128, 2048], "addm": [128, 2048],
                "tri": [128, 512], "ident": [128, 128]}

A_IN = 2048
B_IN = 2 * 1024 + 2 * 4 * 128 + 16
C_IN = 1024 + 6 * 256 + 48
OFF_B = A_IN
OFF_Z = OFF_B
OFF_XBC = OFF_B + 1024
OFF_DT = OFF_XBC + 2048
OFF_C = A_IN + B_IN
OFF_Q = OFF_C
OFF_KC = OFF_Q + 1024
OFF_VC = OFF_KC + 256
OFF_KS = OFF_VC + 256
OFF_VS = OFF_KS + 256
OFF_KW = OFF_VS + 256
OFF_VW = OFF_KW + 256
OFF_GL = OFF_VW + 256
OFF_G = A_IN + B_IN + C_IN


def prep_mixer_inputs(inputs, l, m):
    f = np.float32
    w = inputs["w_in"][l]

    def cw(cols):
        return chunkW(np.ascontiguousarray(w[:, cols]))
    a_val = cw(slice(0, 1024))
    a_gate = cw(slice(1024, 2048))
    m["winA%d" % l] = np.concatenate([a_val, a_gate], axis=2)
    singles = [cw(slice(OFF_Z, OFF_Z + 1024)), cw(slice(OFF_XBC, OFF_XBC + 2048)), cw(slice(OFF_Q, OFF_Q + 1024)),
               cw(slice(OFF_KC, OFF_KC + 256)), cw(slice(OFF_VC, OFF_VC + 256)), cw(slice(OFF_KS, OFF_KS + 256)),
               cw(slice(OFF_KW, OFF_KW + 256))]
    glw = np.zeros((1024, 128), f)
    glw[:, :48] = w[:, OFF_GL:OFF_GL + 48]
    singles.append(chunkW(glw))
    m["winS%d" % l] = np.concatenate(singles, axis=0)
    wv = np.concatenate([w[:, OFF_VS:OFF_VS + 256], w[:, OFF_VW:OFF_VW + 256]], 1)
    m["wvT%d" % l] = np.ascontiguousarray(wv.reshape(8, 128, 512).transpose(1, 0, 2)).reshape(128, 4096)
    wdt = w[:, OFF_DT:OFF_DT + 16]
    m["wdt%d" % l] = np.ascontiguousarray(wdt.reshape(8, 128, 16).transpose(1, 0, 2)).reshape(128, 128)
    m["wgate%d" % l] = chunkW(np.ascontiguousarray(w[:, OFF_G:OFF_G + 3072]))
    m["waout%d" % l] = chunkW(inputs["w_a_out"][l])
    m["wbout%d" % l] = chunkW(inputs["w_b_out"][l])
    m["wcout%d" % l] = chunkW(inputs["w_c_out"][l])
    m["wo%d" % l] = chunkW(inputs["w_o"][l])
    m["convaw%d" % l] = np.ascontiguousarray(inputs["conv_a_w"][l].T.reshape(8, 128, 31).transpose(1, 0, 2)).reshape(128, 248)
    m["convab%d" % l] = colvec(inputs["conv_a_b"][l])
    m["normag%d" % l] = colvec(inputs["norm_a_g"][l])
    m["normab%d" % l] = colvec(inputs["norm_a_b"][l])
    m["ssmcw%d" % l] = np.ascontiguousarray(inputs["ssm_conv_w"][l].T.reshape(16, 128, 4).transpose(1, 0, 2)).reshape(128, 64)
    m["ssmcb%d" % l] = colvec(inputs["ssm_conv_b"][l])
    m["dtb%d" % l] = np.tile(inputs["ssm_dt_bias"][l][None, :], (128, 1))
    m["alog%d" % l] = np.tile(inputs["ssm_a_log"][l][None, :], (128, 1))
    m["ssmd%d" % l] = colvec(np.repeat(inputs["ssm_d"][l], 64))
    m["ssmnw%d" % l] = colvec(inputs["ssm_norm_w"][l])
    for nm, key in (("k", "cmp_pe_k"), ("v", "cmp_pe_v")):
        m["pe%s%d" % (nm, l)] = np.ascontiguousarray(inputs[key][l].T)
    for nm, key in (("k", "cmp_k_w1"), ("v", "cmp_v_w1")):
        m["w1%s%d" % (nm, l)] = np.ascontiguousarray(inputs[key][l].reshape(32, 64, 128).transpose(1, 0, 2)).reshape(64, 4096)
    m["w2k%d" % l] = inputs["cmp_k_w2"][l]
    m["w2v%d" % l] = inputs["cmp_v_w2"][l]


MIX_SHAPES = {"winA": [8, 128, 2048], "winS": [41, 128, 1024], "wvT": [128, 4096], "wdt": [128, 128], "wgate": [24, 128, 1024],
              "waout": [8, 128, 1024], "wbout": [8, 128, 1024], "wcout": [8, 128, 1024], "wo": [8, 128, 1024],
              "convaw": [128, 248], "convab": [128, 8], "normag": [128, 8], "normab": [128, 8], "ssmcw": [128, 64],
              "ssmcb": [128, 16], "dtb": [128, 16], "alog": [128, 16], "ssmd": [128, 8], "ssmnw": [128, 8],
              "pek": [64, 32], "pev": [64, 32], "w1k": [64, 4096], "w1v": [64, 4096], "w2k": [128, 64], "w2v": [128, 64]}


def phase_m1(E, l):
    kb, A, P = E["kb"], E["kb"].A, E["kb"].P
    d = E["lay"][l]
    sc = E["scr"]
    C = E["consts"]
    mo, mob = E["mods"][l]
    m1, m1b = E["mod1"][l]
    hT = E["hT"]
    mk = A.mark()
    kb.nrot = 8
    u, ub = kb.sb([128, 8, S], BF16, "u")
    def ld(name, shape, src):
        t, b = kb.sb(shape, F32, name, dma=True)
        kb.dma(t[:], src.ap(), b, (), [b])
        return t, b
    caw, cawb = ld("caw", [128, 248], d["convaw"])
    cab, cabb = ld("cab", [128, 8], d["convab"])
    scw, scwb = ld("scw", [128, 64], d["ssmcw"])
    scb_, scbb = ld("scb", [128, 16], d["ssmcb"])
    dtb, dtbb = ld("dtb", [128, 16], d["dtb"])
    dt_sb, dt_b = E["dt"]
    mk2 = A.mark()
    hst = [kb.sb([128, 1024], F32, "hst", dma=True) for _ in range(2)]
    i = 0
    for k in range(8):
        for hh in range(4):
            st, stb = hst[i % 2]
            i += 1
            kb.dma(st[:], hT.ap()[k * 128:(k + 1) * 128, hh * 1024:(hh + 1) * 1024], stb, (), [stb])
            kb.act(u[:, k, hh * 1024:(hh + 1) * 1024], st[:], AF.Identity, [stb, mob, m1b], [ub],
                   bias=mo[:, 24 + k:24 + k + 1], scale=m1[:, 32 + k:32 + k + 1])
    wv32, wv32b = kb.sb([128, 4096], F32, "wv32", dma=True)
    kb.dma(wv32[:], d["wvT"].ap(), wv32b, (), [wv32b])
    wvb, wvbb = kb.sb([128, 4096], BF16, "wvb")
    kb.cp("pool", wvb[:], wv32[:], [wv32b], [wvbb])
    wd32, wd32b = kb.sb([128, 128], F32, "wd32", dma=True)
    kb.dma(wd32[:], d["wdt"].ap(), wd32b, (), [wd32b])
    wdb, wdbb = kb.sb([128, 128], BF16, "wdb")
    kb.cp("pool", wdb[:], wd32[:], [wd32b], [wdbb])
    vo = [kb.sb([128, 512], BF16, "vo", dma=True) for _ in range(2)]
    dtt, dttb = kb.sb([128, 16], F32, "dtt")
    for tcn in range(32):
        tsl = slice(tcn * 128, (tcn + 1) * 128)
        pv, pvb = kb.psum()
        for k in range(8):
            kb.mm(pv[:], u[:, k, tsl], wvb[:, k * 512:(k + 1) * 512], k == 0, k == 7, [ub, wvbb], [pvb])
        o, ob = vo[tcn % 2]
        kb.cp("act", o[:], pv[:], [pvb], [ob])
        kb.dma(sc["vsw"].ap()[tsl, :], o[:], ob, [ob], ())
        pd, pdb = kb.psum()
        for k in range(8):
            kb.mm(pd[:, 0:16], u[:, k, tsl], wdb[:, k * 16:(k + 1) * 16], k == 0, k == 7, [ub, wdbb], [pdb])
        kb.tt("dve", dtt[:], pd[:, 0:16], dtb[:], ALU.add, [pdb, dtbb], [dttb])
        kb.act(dtt[:], dtt[:], AF.Exp, [dttb], [dttb])
        kb.act(dt_sb[:, tcn, :], dtt[:], AF.Ln, [dttb], [dt_b], bias=1.0)
    P.barrier()
    A.release(mk2)
    ropec, ropecb = ld("ropec", [128, S], C["ropec"])
    ropes, ropesb = ld("ropes", [128, S], C["ropes"])
    ws = WStream(kb, 2048, 2, "win")
    ga = [kb.sb([128, 30 + S], F32, "ga") for _ in range(2)]
    acc = [kb.sb([128, S], F32, "acc", dma=True) for _ in range(1)]
    accz = [kb.sb([128, S], F32, "accz", dma=True) for _ in range(1)]
    rowb = [kb.sb([128, S], BF16, "rowb", dma=True) for _ in range(1)]
    tmp = [kb.sb([128, 512], F32, "tmp") for _ in range(2)]
    tmp2 = [kb.sb([128, 512], F32, "tmp2") for _ in range(2)]
    for g_, gb_ in ga:
        kb.memset("pool", g_[:, 0:30], 0.0, [gb_])
    others = [("z", c) for c in range(8)] + [("xbc", c) for c in range(16)] + \
             [("q", c) for c in range(8)] + [("kc", c) for c in range(2)] + [("vc", c) for c in range(2)] + \
             [("ks", c) for c in range(2)] + [("kw", c) for c in range(2)] + [("gl", 0)]
    items = []
    oi = 0
    for c in range(8):
        items.append(("A", c))
        for _ in range(5):
            if oi < len(others):
                items.append(others[oi])
                oi += 1
    items += others[oi:]

    def src_of(it):
        kind, c = it
        if kind == "A":
            return d["winA"].ap()[c], 2048
        base = {"z": 0, "xbc": 8, "q": 24, "kc": 32, "vc": 34, "ks": 36, "kw": 38, "gl": 40}[kind]
        return d["winS"].ap()[base + c], 1024
    s0, n0 = src_of(items[0])
    ws.prefetch(s0, n0)
    cnt = {"ga": 0, "acc": 0, "rowb": 0, "tmp": 0, "tmp2": 0}

    def nxt(lst, key):
        r = lst[cnt[key] % len(lst)]
        cnt[key] += 1
        return r

    def rope_tile(ps, psb, out_ap, outb, t0):
        x, xb = nxt(tmp, "tmp")
        kb.cp("act", x[:], ps[:], [psb], [xb])
        t1, t1b = nxt(tmp2, "tmp2")
        kb.tt("dve", t1[:], x[:], ropec[:, t0:t0 + 512], ALU.mult, [xb, ropecb], [t1b])
        t2, t2b = nxt(tmp, "tmp")
        for q4 in range(4):
            src = (q4 ^ 1) * 32
            kb.tt("pool", t2[q4 * 32:(q4 + 1) * 32, :], x[src:src + 32, :], ropes[src:src + 32, t0:t0 + 512],
                  ALU.mult, [xb, ropesb], [t2b])
        kb.tt("dve", out_ap, t1[:], t2[:], ALU.add, [t1b, t2b], [outb])

    for ii, it in enumerate(items):
        kind, c = it
        if ii + 1 < len(items):
            s1, n1 = src_of(items[ii + 1])
            ws.prefetch(s1, n1)
        w, wb = ws.get("pool")
        if kind == "A":
            g_, gb_ = nxt(ga, "ga")
            for t in range(8):
                t0 = t * 512
                pv, pvb = kb.psum()
                pg, pgb = kb.psum()
                for k in range(8):
                    kb.mm(pv[:], w[:, k * 128:(k + 1) * 128], u[:, k, t0:t0 + 512], k == 0, k == 7, [wb, ub], [pvb])
                for k in range(8):
                    kb.mm(pg[:], w[:, 1024 + k * 128:1024 + (k + 1) * 128], u[:, k, t0:t0 + 512], k == 0, k == 7, [wb, ub], [pgb])
                x, xb = nxt(tmp, "tmp")
                kb.act(x[:], pg[:], AF.Sigmoid, [pgb], [xb])
                kb.tt("dve", g_[:, 30 + t0:30 + t0 + 512], x[:], pv[:], ALU.mult, [xb, pvb], [gb_])
            a_, ab_ = nxt(acc, "acc")
            kb.ts("dve", a_[:], g_[:, 0:S], caw[:, c * 31:c * 31 + 1], cab[:, c:c + 1], ALU.mult, ALU.add, [gb_, cawb, cabb], [ab_])
            for k in range(1, 31):
                kb.stt(a_[:], g_[:, k:k + S], caw[:, c * 31 + k:c * 31 + k + 1], a_[:], ALU.mult, ALU.add, [gb_, cawb, ab_], [ab_])
            kb.dma(sc["aconvT"].ap()[c * 128:(c + 1) * 128, :], a_[:], ab_, [ab_], ())
            continue
        if kind == "xbc":
            g_, gb_ = nxt(ga, "ga")
        elif kind in ("z", "gl"):
            a_, ab_ = accz[0]
        else:
            r_, rb_ = nxt(rowb, "rowb")
        for t in range(8):
            t0 = t * 512
            pv, pvb = kb.psum()
            for k in range(8):
                kb.mm(pv[:], w[:, k * 128:(k + 1) * 128], u[:, k, t0:t0 + 512], k == 0, k == 7, [wb, ub], [pvb])
            if kind == "z":
                kb.act(a_[:, t0:t0 + 512], pv[:], AF.Silu, [pvb], [ab_])
            elif kind == "gl":
                kb.act(a_[:, t0:t0 + 512], pv[:], AF.Sigmoid, [pvb], [ab_])
            elif kind == "xbc":
                kb.cp("act", g_[:, 30 + t0:30 + t0 + 512], pv[:], [pvb], [gb_])
            elif kind in ("kc", "vc"):
                kb.cp("act", r_[:, t0:t0 + 512], pv[:], [pvb], [rb_])
            else:
                rope_tile(pv, pvb, r_[:, t0:t0 + 512], rb_, t0)
        if kind == "z":
            kb.dma(sc["zsT"].ap()[c * 128:(c + 1) * 128, :], a_[:], ab_, [ab_], ())
        elif kind == "gl":
            kb.dma(sc["glT"].ap(), a_[:], ab_, [ab_], ())
        elif kind == "xbc":
            a_, ab_ = nxt(acc, "acc")
            kb.ts("dve", a_[:], g_[:, 27:27 + S], scw[:, c * 4:c * 4 + 1], scb_[:, c:c + 1], ALU.mult, ALU.add, [gb_, scwb, scbb], [ab_])
            for k in range(1, 4):
                kb.stt(a_[:], g_[:, 27 + k:27 + k + S], scw[:, c * 4 + k:c * 4 + k + 1], a_[:], ALU.mult, ALU.add, [gb_, scwb, ab_], [ab_])
            kb.act(a_[:], a_[:], AF.Silu, [ab_], [ab_])
            dst = sc["xsT"].ap()[c * 128:(c + 1) * 128, :] if c < 8 else (
                sc["BT"].ap()[(c - 8) * 128:(c - 7) * 128, :] if c < 12 else sc["CT"].ap()[(c - 12) * 128:(c - 11) * 128, :])
            kb.dma(dst, a_[:], ab_, [ab_], ())
        else:
            dst = {"q": sc["qT"], "kc": sc["kcT"], "vc": sc["vcT"], "ks": sc["ksT"], "kw": sc["kwT"]}[kind]
            kb.dma(dst.ap()[c * 128:(c + 1) * 128, :], r_[:], rb_, [rb_], ())
    P.barrier()
    A.release(mk)


def phase_ssd(E, l):
    kb, A, P = E["kb"], E["kb"].A, E["kb"].P
    d = E["lay"][l]
    sc = E["scr"]
    C = E["consts"]
    mk = A.mark()
    kb.nrot = 8
    dt_sb, dt_b = E["dt"]

    def ld(name, shape, src):
        t, b = kb.sb(shape, F32, name, dma=True)
        o = t[:]
        if len(shape) == 3:
            o = o.rearrange("p a b -> p (a b)")
        kb.dma(o, src.ap(), b, (), [b])
        return t, b
    tri, trib = ld("tri", [128, 2, 256], C["tri"])
    ident, identb = ld("ident", [128, 128], C["ident"])
    alog, alogb = ld("alog", [128, 16], d["alog"])
    Dc, Dcb = ld("Dc", [128, 8], d["ssmd"])
    negA, negAb = kb.sb([128, 16], F32, "negA")
    kb.act(negA[:], alog[:], AF.Exp, [alogb], [negAb])
    kb.ts("dve", negA[:], negA[:], -1.0, None, ALU.mult, None, [negAb], [negAb])
    H, Hb_ = kb.sb([128, 1024], F32, "H")
    Hbf, Hbfb = kb.sb([128, 1024], BF16, "Hbf")
    kb.memset("dve", H[:], 0.0, [Hb_])
    kb.memset("dve", Hbf[:], 0.0, [Hbfb])

    def two(shape, dt_, name, dma=False):
        return [kb.sb(shape, dt_, name, dma=dma) for _ in range(2)]
    xs_s = two([128, 8, 256], F32, "xs", True)
    zs1 = kb.sb([128, 8, 256], F32, "zs", dma=True)
    Bt_s = two([128, 4, 256], F32, "Bt", True)
    Ct_s = two([128, 4, 256], F32, "Ct", True)
    Btb_s = two([128, 4, 256], BF16, "Btb")
    Ctb_s = two([128, 4, 256], BF16, "Ctb")
    a_t_s = two([128, 2, 16], F32, "a_t")
    acsT_s = two([128, 2, 16], F32, "acsT")
    abc, abcb = kb.sb([128, 2, 16, 128], F32, "abc")
    acsR_s = two([128, 16, 256], F32, "acsR")
    eR_s = two([128, 16, 256], F32, "eR")
    di_s = two([128, 2, 16], F32, "di")
    dtd_s = two([128, 2, 16], F32, "dtd")
    cbm_s = two([128, 4, 2, 256], F32, "cbm")
    xp_s = two([128, 2, 1024], BF16, "xp")
    xpd_s = two([128, 2, 1024], BF16, "xpd")
    Bn_s = two([128, 2, 512], BF16, "Bn")
    NSL = 3
    dif = [kb.sb([128, 256], F32, "dif") for _ in range(2 * NSL)]
    Mt = [kb.sb([128, 2, 256], BF16, "Mt") for _ in range(NSL)]
    Cs = [kb.sb([128, 256], BF16, "Cs") for _ in range(NSL)]
    yo = two([128, 8, 256], F32, "yo", True)
    for (m_, mb_) in Mt:
        kb.memset("pool", m_[:], 0.0, [mb_])

    def load(c):
        s = c % 2
        csl = slice(c * 256, (c + 1) * 256)
        for k in range(8):
            kb.dma(xs_s[s][0][:, k, :], sc["xsT"].ap()[k * 128:(k + 1) * 128, csl], xs_s[s][1], (), [xs_s[s][1]])
        for g in range(4):
            kb.dma(Bt_s[s][0][:, g, :], sc["BT"].ap()[g * 128:(g + 1) * 128, csl], Bt_s[s][1], (), [Bt_s[s][1]])
            kb.dma(Ct_s[s][0][:, g, :], sc["CT"].ap()[g * 128:(g + 1) * 128, csl], Ct_s[s][1], (), [Ct_s[s][1]])

    def preamble(c):
        s = c % 2
        xs, xsb = xs_s[s]
        Bt, Btb_ = Bt_s[s]
        Ct, Ctb_ = Ct_s[s]
        Btb, Btbb = Btb_s[s]
        Ctb, Ctbb = Ctb_s[s]
        a_t, a_tb = a_t_s[s]
        acsT, acsTb = acsT_s[s]
        acsR, acsRb = acsR_s[s]
        eR, eRb = eR_s[s]
        di, dib = di_s[s]
        dtd, dtdb = dtd_s[s]
        cbm, cbmb = cbm_s[s]
        xp, xpb = xp_s[s]
        xpd, xpdb = xpd_s[s]
        Bn, Bnb = Bn_s[s]
        for lt in range(2):
            kb.tt("dve", a_t[:, lt, :], dt_sb[:, 2 * c + lt, :], negA[:], ALU.mult, [dt_b, negAb], [a_tb])
        for lo in range(2):
            ps, psb = kb.psum()
            for lt in range(lo + 1):
                kb.mm(ps[:, 0:16], tri[:, lt, lo * 128:(lo + 1) * 128], a_t[:, lt, :], lt == 0, lt == lo, [trib, a_tb], [psb])
            kb.cp("act", acsT[:, lo, :], ps[:, 0:16], [psb], [acsTb])
        for lt in range(2):
            kb.cp("pool", abc[:, lt, :, :], a_t[:, lt, :].unsqueeze(2).to_broadcast([128, 16, 128]), [a_tb], [abcb])
        yield
        for hp in range(8):
            ps, psb = kb.psum()
            for hh in range(2):
                h = 2 * hp + hh
                for lt in range(2):
                    kb.mm(ps[:, hh * 256:(hh + 1) * 256], abc[:, lt, h, :], tri[:, lt, :], lt == 0, lt == 1, [abcb, trib], [psb])
            kb.cp("act", acsR[:, 2 * hp:2 * hp + 2, :], ps[:].rearrange("p (a b) -> p a b", a=2), [psb], [acsRb])
            yield
        kb.act(eR[:], acsR[:], AF.Exp, [acsRb], [eRb])
        for lt in range(2):
            kb.tt("dve", di[:, lt, :], acsR[:, :, 255], acsT[:, lt, :], ALU.subtract, [acsRb, acsTb], [dib])
        kb.act(di[:], di[:], AF.Exp, [dib], [dib])
        for lt in range(2):
            kb.tt("dve", dtd[:, lt, :], di[:, lt, :], dt_sb[:, 2 * c + lt, :], ALU.mult, [dib, dt_b], [dtdb])
        yield
        kb.cp("pool", Btb[:], Bt[:], [Btb_], [Btbb])
        kb.cp("pool", Ctb[:], Ct[:], [Ctb_], [Ctbb])
        for g in range(4):
            for st in range(2):
                ps, psb = kb.psum()
                kb.mm(ps[:, 0:256], Btb[:, g, st * 128:(st + 1) * 128], Ctb[:, g, :], True, True, [Btbb, Ctbb], [psb])
                kb.tt("dve", cbm[:, g, st, :], ps[:, 0:256], tri[:, st, :], ALU.mult, [psb, trib], [cbmb])
        yield
        for k in range(8):
            for lt in range(2):
                ps, psb = kb.psum()
                kb.tr(ps[:, 0:128], xs[:, k, lt * 128:(lt + 1) * 128], ident[:], [xsb, identb], [psb])
                for hh in range(2):
                    h = 2 * k + hh
                    kb.act(xp[:, lt, h * 64:(h + 1) * 64], ps[:, hh * 64:(hh + 1) * 64], AF.Identity, [psb, dt_b], [xpb],
                           scale=dt_sb[:, 2 * c + lt, h:h + 1])
                    kb.ts("dve", xpd[:, lt, h * 64:(h + 1) * 64], ps[:, hh * 64:(hh + 1) * 64], dtd[:, lt, h:h + 1], None,
                          ALU.mult, None, [psb, dtdb], [xpdb])
            if k % 2 == 1:
                yield
        for g in range(4):
            for lt in range(2):
                ps, psb = kb.psum()
                kb.tr(ps[:, 0:128], Bt[:, g, lt * 128:(lt + 1) * 128], ident[:], [Btb_, identb], [psb])
                kb.cp("act", Bn[:, lt, g * 128:(g + 1) * 128], ps[:, 0:128], [psb], [Bnb])
        yield

    def run_all(gen):
        for _ in gen:
            pass

    load(0)
    run_all(preamble(0))
    cnt = [0]
    for c in range(16):
        s = c % 2
        gen = None
        if c + 1 < 16:
            load(c + 1)
            gen = preamble(c + 1)
            if SSDVAR == 1:
                run_all(gen)
                gen = None
        zs, zsb = zs1
        csl = slice(c * 256, (c + 1) * 256)
        for k in range(8):
            kb.dma(zs[:, k, :], sc["zsT"].ap()[k * 128:(k + 1) * 128, csl], zsb, (), [zsb])
        xs, xsb = xs_s[s]
        Ct, Ctb_ = Ct_s[s]
        acsT, acsTb = acsT_s[s]
        acsR, acsRb = acsR_s[s]
        eR, eRb = eR_s[s]
        cbm, cbmb = cbm_s[s]
        xp, xpb = xp_s[s]
        xpd, xpdb = xpd_s[s]
        Bn, Bnb = Bn_s[s]
        y_, yb_ = yo[s]
        for h in range(16):
            g = h // 4
            i = cnt[0]
            cnt[0] += 1
            m_, mb_ = Mt[i % NSL]
            for st in range(2):
                lo = st * 128
                df, dfb = dif[(i % NSL) * 2 + st]
                kb.ts("dve", df[:, lo:256], acsR[:, h, lo:256], acsT[:, st, h:h + 1], 0.0, ALU.subtract, ALU.min,
                      [acsRb, acsTb], [dfb])
                kb.act(df[:, lo:256], df[:, lo:256], AF.Exp, [dfb], [dfb])
                kb.tt("pool", m_[:, st, lo:256], df[:, lo:256], cbm[:, g, st, lo:256], ALU.mult, [dfb, cbmb], [mb_])
            cs, csb = Cs[i % NSL]
            kb.tt("pool", cs[:], Ct[:, g, :], eR[:, h, :], ALU.mult, [Ctb_, eRb], [csb])
            ps, psb = kb.psum()
            kb.mm(ps[0:64, 0:256], xp[:, 0, h * 64:(h + 1) * 64], m_[:, 0, :], True, False, [xpb, mb_], [psb])
            kb.mm(ps[0:64, 0:256], xp[:, 1, h * 64:(h + 1) * 64], m_[:, 1, :], False, False, [xpb, mb_], [psb])
            kb.mm(ps[0:64, 0:256], Hbf[:, h * 64:(h + 1) * 64], cs[:], False, True, [Hbfb, csb], [psb])
            hh = h % 2
            kb.cp("act", y_[hh * 64:(hh + 1) * 64, h // 2, :], ps[0:64, 0:256], [psb], [yb_])
            if gen is not None and SSDVAR == 0:
                next(gen, None)
        if gen is not None:
            run_all(gen)
        for k in range(8):
            kb.stt(y_[:, k, :], xs[:, k, :], Dc[:, k:k + 1], y_[:, k, :], ALU.mult, ALU.add, [xsb, Dcb, yb_], [yb_])
            kb.tt("pool", y_[:, k, :], y_[:, k, :], zs[:, k, :], ALU.mult, [yb_, zsb], [yb_])
            kb.dma(sc["ybT"].ap()[k * 128:(k + 1) * 128, csl], y_[:, k, :], yb_, [yb_], ())
        for g in range(4):
            ps, psb = kb.psum()
            for lt in range(2):
                kb.mm(ps[:, 0:256], Bn[:, lt, g * 128:(g + 1) * 128], xpd[:, lt, g * 256:(g + 1) * 256], lt == 0, lt == 1,
                      [Bnb, xpdb], [psb])
            for r in range(4):
                h = 4 * g + r
                kb.stt(H[:, h * 64:(h + 1) * 64], H[:, h * 64:(h + 1) * 64], eR[:, h, 255:256], ps[:, r * 64:(r + 1) * 64],
                       ALU.mult, ALU.add, [Hb_, eRb, psb], [Hb_])
        kb.cp("pool", Hbf[:], H[:], [Hb_], [Hbfb])
    P.barrier()
    A.release(mk)


GELU_C = math.sqrt(2.0 / math.pi)


def phase_nsa(E, l):
    kb, A, P = E["kb"], E["kb"].A, E["kb"].P
    d = E["lay"][l]
    sc = E["scr"]
    C = E["consts"]
    ones, onesb = E["ones"]
    mk = A.mark()
    kb.nrot = 5
    PO = [(kb.ps[5], kb.psb[5]), (kb.ps[6], kb.psb[6]), (kb.ps[7], kb.psb[7])]
    poi = [0]

    def next_po():
        r = PO[poi[0] % 3]
        poi[0] += 1
        return r

    def ld(name, shape, src, dtype=F32):
        t, b = kb.sb(shape, dtype, name, dma=True)
        o = t[:]
        if len(shape) == 3:
            o = o.rearrange("p a b -> p (a b)")
        kb.dma(o, src if not hasattr(src, "ap") else src.ap(), b, (), [b])
        return t, b

    def ld_bf(name, shape, src, stg, dest):
        t, b = dest
        o = t[:]
        if len(shape) == 3:
            o = o.rearrange("p a b -> p (a b)")
        npart = shape[0]
        n = int(np.prod(shape[1:]))
        st, stb = stg
        for c0 in range(0, n, 4096):
            c1 = min(n, c0 + 4096)
            kb.dma(st[0:npart, 0:c1 - c0], src.ap()[:, c0:c1], stb, (), [stb])
            kb.cp("pool", o[:, c0:c1], st[0:npart, 0:c1 - c0], [stb], [b])
        return t, b

    ident, identb = ld("ident", [128, 128], C["ident"])
    wm_d = kb.sb([128, 8, 512], BF16, "wm")
    cmT_d = kb.sb([128, 2, S], BF16, "cmT")
    eall_d = kb.sb([64, S], BF16, "eall")
    ov_d = kb.sb([128, 2, 65], BF16, "ov")
    w1_d = {"k": kb.sb([64, 32, 128], BF16, "w1k"), "v": kb.sb([64, 32, 128], BF16, "w1v")}
    mk_stg = A.mark()
    stg = kb.sb([128, 4096], F32, "stg", dma=True)
    wm, wmb = ld_bf("wm", [128, 8, 512], C["wm"], stg, wm_d)
    cmT, cmTb = ld_bf("cmT", [128, 2, S], C["cmaskT"], stg, cmT_d)
    eall, eallb = ld_bf("eall", [64, S], C["eall"], stg, eall_d)
    ovb, ovbb = ld_bf("ov", [128, 2, 65], C["ovaug"], stg, ov_d)
    w1 = {}
    for nm in ("k", "v"):
        w1[nm] = ld_bf("w1" + nm, [64, 32, 128], d["w1" + nm], stg, w1_d[nm])
    P.barrier()
    A.release(mk_stg)
    cmpc, cmpcb = ld("cmpc", [64, 256], C["cmpc"])
    cmps, cmpsb = ld("cmps", [64, 256], C["cmps"])
    gl_s = [kb.sb([48, 512], F32, "gl", dma=True) for _ in range(2)]
    selg, selgb = kb.sb([112, 48, 64], BF16, "selg")
    kb.memset("dve", selg[:], 0.0, [selgb])
    kb.cp("dve", selg[0:48], ident[0:48, 0:48].unsqueeze(2).to_broadcast([48, 48, 64]), [identb], [selgb])
    kb.cp("dve", selg[64:112], ident[64:112, 64:112].unsqueeze(2).to_broadcast([48, 48, 64]), [identb], [selgb])
    w2 = {}
    pe = {}
    for nm in ("k", "v"):
        t32, t32b = ld("w2" + nm, [128, 64], d["w2" + nm])
        tb, tbb = kb.sb([128, 64], BF16, "w2b")
        kb.cp("pool", tb[:], t32[:], [t32b], [tbb])
        w2[nm] = (tb, tbb)
        p32, p32b = ld("pe" + nm, [64, 32], d["pe" + nm])
        pb, pbb = kb.sb([64, 32], BF16, "peb")
        kb.cp("pool", pb[:], p32[:], [p32b], [pbb])
        pe[nm] = (pb, pbb)
    ks, ksb = kb.sb([128, S], BF16, "ks", dma=True)
    kw, kwb = kb.sb([128, S], BF16, "kw", dma=True)
    vs, vsb = kb.sb([128, 32, 128], BF16, "vs", dma=True)
    vw, vwb = kb.sb([128, 32, 128], BF16, "vw", dma=True)
    kc, kcb = kb.sb([64, S], BF16, "kc", dma=True)
    vc, vcb = kb.sb([64, S], BF16, "vc", dma=True)
    kcmp, kcmpb = kb.sb([128, 256], BF16, "kcmp")
    vcmp, vcmpb = kb.sb([128, 2, 128], BF16, "vcmp")
    h1 = [kb.sb([128, 256], F32, "h1") for _ in range(2)]
    h1b_ = kb.sb([128, 256], BF16, "h1b")
    gtmp = [kb.sb([128, 256], F32, "gt") for _ in range(2)]
    biasc, biascb = kb.sb([128, 1], F32, "biasc")
    qt_s = [kb.sb([128, 2, 512], BF16, "qt", dma=True) for _ in range(2)]
    Es = [kb.sb([128, 512], BF16, "E") for _ in range(14)]
    ei = [0]
    selm, selmb = kb.sb([128, 32, 512], BF16, "selm")
    selT, selTb = kb.sb([64, 512], BF16, "selT")
    impa, impab = kb.sb([128, 4, 64], F32, "impa")
    imps, impsb = kb.sb([128, 64], F32, "imps")
    imp2, imp2b = kb.sb([128, 64], F32, "imp2")
    sel4, sel4b = kb.sb([128, 4, 64], F32, "sel4")
    m8, m8b = kb.sb([128, 8], F32, "m8")
    thr, thrb = kb.sb([128, 1], F32, "thr")
    rden, rdenb = kb.sb([128, 1], F32, "rden")
    gl2_s = [kb.sb([112, 512], BF16, "gl2") for _ in range(2)]
    for g2_, g2b_ in gl2_s:
        kb.memset("dve", g2_[:], 0.0, [g2b_])
    gld, gldb = kb.sb([48, 512], F32, "gld")
    wgt, wgtb = kb.sb([64, 512], F32, "wgt")
    tiny, tinyb = kb.sb([128, 1], F32, "tiny")
    kb.memset("dve", tiny[:], 1e-30, [tinyb])
    otmp, otmpb = kb.sb([64, 512], F32, "otmp")
    oaccs = [kb.sb([64, 512], F32, "oacc") for _ in range(4)]
    keep_s = [kb.sb([128, 4, 64], F32, "keep", dma=True) for _ in range(2)]
    addm_s = [kb.sb([128, 4, 64], F32, "addm", dma=True) for _ in range(2)]
    obuf = [kb.sb([128, 512], BF16, "obuf", dma=True) for _ in range(2)]
    kb.memset("dve", vs[:, :, 64:128], 1.0, [vsb])
    kb.memset("dve", vw[:, :, 64:128], 1.0, [vwb])
    kb.memset("dve", vcmp[:], 0.0, [vcmpb])
    kb.memset("dve", vcmp[:, 0, 64:128], 1.0, [vcmpb])
    kb.memset("dve", vcmp[0:127, 1, 64:128], 1.0, [vcmpb])
    kb.memset("dve", kcmp[:], 0.0, [kcmpb])

    def gelu_tanh(x, xb, out, outb):
        t, tb = gtmp[0]
        t2, t2b = gtmp[1]
        kb.tt("dve", t[:], x[:], x[:], ALU.mult, [xb], [tb])
        kb.ts("dve", t[:], t[:], 0.044715, 1.0, ALU.mult, ALU.add, [tb], [tb])
        kb.tt("dve", t[:], t[:], x[:], ALU.mult, [tb, xb], [tb])
        kb.act(t2[:], t[:], AF.Tanh, [tb], [t2b], scale=GELU_C)
        kb.ts("dve", t2[:], t2[:], 1.0, 0.5, ALU.add, ALU.mult, [t2b], [t2b])
        kb.tt("dve", out, t2[:], x[:], ALU.mult, [t2b, xb], [outb])

    LA = 10
    pipe = {"q": [], "step": 0}

    def p_add(delay, fn):
        pipe["q"].append((pipe["step"] + delay, fn))

    def p_tick():
        pipe["step"] += 1
        rest = []
        for due, fn in pipe["q"]:
            if due <= pipe["step"]:
                fn()
            else:
                rest.append((due, fn))
        pipe["q"] = rest

    def p_flush():
        while pipe["q"]:
            p_tick()

    def attend(kT, kTb, half, kcol, vaug_ap, vb, mask_ap, maskb, q_ap, qb, po, pob, first, last, mi):
        ps, psb = kb.psum()
        kb.mm(ps[:], kT[half * 64:(half + 1) * 64, kcol:kcol + 128], q_ap, True, True, [kTb, qb], [psb])
        e, eb = Es[ei[0] % len(Es)]
        ei[0] += 1
        kb.act(e[:], ps[:], AF.Exp, [psb], [eb], scale=0.125)
        kb.tt("pool" if mi % 3 == 2 else "dve", e[:], e[:], mask_ap, ALU.mult, [eb, maskb], [eb])
        p_add(LA, lambda: kb.mm(po[:, :], vaug_ap, e[:], first, last, [vb, eb], [pob]))
        p_tick()
        return e, eb

    def finish_branch(po, pob, grow, first, glt, oacc, oaccb):
        gl2, gl2b = glt

        def rest():
            kb.act(wgt[:], po[64:128, :], AF.Ln, [pob, tinyb], [wgtb], bias=tiny[64:128, 0:1])
            kb.act(wgt[:], wgt[:], AF.Exp, [wgtb], [wgtb], scale=-1.0)
            pg, pgb = kb.psum()
            kb.mm(pg[0:64, :], selg[:, grow, :], gl2[:, :], True, True, [selgb, gl2b], [pgb])
            kb.tt("dve", wgt[:], wgt[:], pg[0:64, :], ALU.mult, [wgtb, pgb], [wgtb])
            if first:
                kb.tt("dve", oacc[:], po[0:64, :], wgt[:], ALU.mult, [pob, wgtb], [oaccb])
            else:
                kb.tt("dve", otmp[:], po[0:64, :], wgt[:], ALU.mult, [pob, wgtb], [otmpb])
                kb.tt("pool", oacc[:], oacc[:], otmp[:], ALU.add, [oaccb, otmpb], [oaccb])
        p_add(LA + 1, rest)
        p_tick()

    qi = 0
    for g in range(4):
        gr = slice(g * 64, (g + 1) * 64)
        for hf in range(2):
            kb.dma(ks[hf * 64:(hf + 1) * 64, :], sc["ksT"].ap()[gr, :], ksb, (), [ksb])
            kb.dma(kw[hf * 64:(hf + 1) * 64, :], sc["kwT"].ap()[gr, :], kwb, (), [kwb])
        kb.dma(kc[:], sc["kcT"].ap()[gr, :], kcb, (), [kcb])
        kb.dma(vc[:], sc["vcT"].ap()[gr, :], vcb, (), [vcb])
        for ch in range(0, 32, 8):
            kb.dma(vs[:, ch:ch + 8, 0:64], sc["vsw"].ap()[ch * 128:(ch + 8) * 128, g * 64:(g + 1) * 64].rearrange("(c p) d -> p c d", p=128),
                   vsb, (), [vsb])
            kb.dma(vw[:, ch:ch + 8, 0:64], sc["vsw"].ap()[ch * 128:(ch + 8) * 128, 256 + g * 64:256 + (g + 1) * 64].rearrange("(c p) d -> p c d", p=128),
                   vwb, (), [vwb])
        for nm, src, srcb in (("k", kc, kcb), ("v", vc, vcb)):
            w1t, w1b_ = w1[nm]
            pb, pbb = pe[nm]
            w2t, w2b_ = w2[nm]
            pbias, pbiasb = kb.psum()
            for j in range(32):
                kb.mm(pbias[:, 0:1], w1t[:, j, :], pb[:, j:j + 1], j == 0, j == 31, [w1b_, pbb], [pbiasb])
            kb.cp("act", biasc[:], pbias[:, 0:1], [pbiasb], [biascb])
            ph, phb = kb.psum()
            for j in range(32):
                kb.mm(ph[:, 0:255], w1t[:, j, :], src[:, j:j + 16 * 254 + 1:16], j == 0, j == 31, [w1b_, srcb], [phb])
            x, xb = h1[0]
            kb.act(x[:, 0:255], ph[:, 0:255], AF.Identity, [phb, biascb], [xb], bias=biasc[:, 0:1])
            hb, hbb = h1b_
            gelu_tanh_in = (x, xb)
            kb.memset("pool", hb[:, 255:256], 0.0, [hbb])
            t, tb = gtmp[0]
            t2, t2b = gtmp[1]
            kb.tt("dve", t[:, 0:255], x[:, 0:255], x[:, 0:255], ALU.mult, [xb], [tb])
            kb.ts("dve", t[:, 0:255], t[:, 0:255], 0.044715, 1.0, ALU.mult, ALU.add, [tb], [tb])
            kb.tt("dve", t[:, 0:255], t[:, 0:255], x[:, 0:255], ALU.mult, [tb, xb], [tb])
            kb.act(t2[:, 0:255], t[:, 0:255], AF.Tanh, [tb], [t2b], scale=GELU_C)
            kb.ts("dve", t2[:, 0:255], t2[:, 0:255], 1.0, 0.5, ALU.add, ALU.mult, [t2b], [t2b])
            kb.tt("dve", hb[:, 0:255], t2[:, 0:255], x[:, 0:255], ALU.mult, [t2b, xb], [hbb])
            if nm == "k":
                pk, pkb = kb.psum()
                kb.mm(pk[0:64, 0:255], w2t[:], hb[:, 0:255], True, True, [w2b_, hbb], [pkb])
                y, yb = h1[1]
                kb.cp("act", y[0:64, 0:255], pk[0:64, 0:255], [pkb], [yb])
                t, tb = gtmp[0]
                t2, t2b = gtmp[1]
                kb.tt("dve", t[0:64, 0:255], y[0:64, 0:255], cmpc[:, 0:255], ALU.mult, [yb, cmpcb], [tb])
                kb.tt("pool", t2[0:32, 0:255], y[32:64, 0:255], cmps[32:64, 0:255], ALU.mult, [yb, cmpsb], [t2b])
                kb.tt("pool", t2[32:64, 0:255], y[0:32, 0:255], cmps[0:32, 0:255], ALU.mult, [yb, cmpsb], [t2b])
                kb.tt("dve", kcmp[0:64, 0:255], t[0:64, 0:255], t2[0:64, 0:255], ALU.add, [tb, t2b], [kcmpb])
                kb.cp("pool", kcmp[64:128, 0:255], kcmp[0:64, 0:255], [kcmpb], [kcmpb])
            else:
                for nt in range(2):
                    nn = 128 if nt == 0 else 127
                    pv, pvb = kb.psum()
                    kb.mm(pv[0:nn, 0:64], hb[:, nt * 128:nt * 128 + nn], w2t[:], True, True, [hbb, w2b_], [pvb])
                    kb.cp("act", vcmp[0:nn, nt, 0:64], pv[0:nn, 0:64], [pvb], [vcmpb])
        for qt in range(8):
            t0 = qt * 512
            q_, qb_ = qt_s[qi % 2]
            qi += 1
            for cc2 in range(2):
                kb.dma(q_[:, cc2, :], sc["qT"].ap()[(g * 2 + cc2) * 128:(g * 2 + cc2 + 1) * 128, t0:t0 + 512], qb_, (), [qb_])
            nnt = 2 if qt >= 4 else 1
            gl32, gl32b = gl_s[qi % 2]
            kb.dma(gl32[:], sc["glT"].ap()[0:48, t0:t0 + 512], gl32b, (), [gl32b])
            gl2, gl2b = gl2_s[qi % 2]
            kb.cp("pool", gl2[0:48, :], gl32[:], [gl32b], [gl2b])
            kb.tt("pool", gld[:], gl32[:], gl2[0:48, :], ALU.subtract, [gl32b, gl2b], [gldb])
            kb.cp("pool", gl2[64:112, :], gld[:], [gldb], [gl2b])
            glt = (gl2, gl2b)
            keep, keepb = keep_s[qi % 2]
            addm, addmb = addm_s[qi % 2]
            kb.dma(keep[:].rearrange("p a b -> p (a b)"), C["keep"].ap()[:, qt * 256:(qt + 1) * 256], keepb, (), [keepb])
            kb.dma(addm[:].rearrange("p a b -> p (a b)"), C["addm"].ap()[:, qt * 256:(qt + 1) * 256], addmb, (), [addmb])
            for r in range(4):
                half = r % 2
                q_ap = q_[half * 64:(half + 1) * 64, r // 2, :]
                po, pob = next_po()
                es = []
                for nt in range(nnt):
                    e, eb = attend(kcmp, kcmpb, half, nt * 128, vcmp[:, nt, :], vcmpb, cmT[:, nt, t0:t0 + 512], cmTb,
                                   q_ap, qb_, po, pob, nt == 0, nt == nnt - 1, nt)
                    es.append((e, eb))

                def imp_block(es=es, r=r, nnt=nnt):
                    for sub in range(4):
                        pi_, pib = kb.psum()
                        for nt in range(nnt):
                            e, eb = es[nt]
                            kb.mm(pi_[:, 0:65], e[:, sub * 128:(sub + 1) * 128], ovb[:, nt, :], nt == 0, nt == nnt - 1, [eb, ovbb], [pib])
                        kb.ts("dve", rden[:], pi_[:, 64:65], 1e-30, None, ALU.max, None, [pib], [rdenb])
                        kb.recip(rden[:], rden[:], [rdenb], [rdenb])
                        if r == 0:
                            kb.ts("dve", impa[:, sub, :], pi_[:, 0:64], rden[:, 0:1], None, ALU.mult, None, [pib, rdenb], [impab])
                        else:
                            kb.stt(impa[:, sub, :], pi_[:, 0:64], rden[:, 0:1], impa[:, sub, :], ALU.mult, ALU.add, [pib, rdenb, impab], [impab])
                p_add(LA, imp_block)
                p_tick()
                finish_branch(po, pob, (g * 4 + r) * 3 + 0, True, glt, oaccs[r][0], oaccs[r][1])
            def topk_chain(keep=keep, keepb=keepb, addm=addm, addmb=addmb):
                for sub in range(4):
                    kb.tt("dve", imps[:], impa[:, sub, :], keep[:, sub, :], ALU.mult, [impab, keepb], [impsb])
                    kb.tt("dve", imps[:], imps[:], addm[:, sub, :], ALU.add, [impsb, addmb], [impsb])
                    P.op("dve", lambda e: e.max(out=m8[:], in_=imps[:]), [impsb], [m8b])
                    P.op("dve", lambda e: e.tensor_reduce(out=thr[:], in_=m8[:], axis=AX.X, op=ALU.min), [m8b], [thrb])
                    kb.ts("dve", imp2[:], imps[:], thr[:, 0:1], 1e6, ALU.is_ge, ALU.mult, [impsb, thrb], [imp2b])
                    kb.tt("dve", imp2[:], imps[:], imp2[:], ALU.subtract, [impsb, imp2b], [imp2b])
                    P.op("dve", lambda e: e.max(out=m8[:], in_=imp2[:]), [imp2b], [m8b])
                    P.op("dve", lambda e: e.tensor_reduce(out=thr[:], in_=m8[:], axis=AX.X, op=ALU.min), [m8b], [thrb])
                    kb.ts("dve", sel4[:, sub, :], imps[:], thr[:, 0:1], None, ALU.is_ge, None, [impsb, thrb], [sel4b])
            p_add(LA + 1, topk_chain)
            p_tick()
            nkc = 4 * qt + 4
            for r in range(4):
                half = r % 2
                hh = g * 4 + r
                q_ap = q_[half * 64:(half + 1) * 64, r // 2, :]
                po, pob = next_po()
                lo = max(0, 4 * qt - 4)
                for kc_ in range(lo, nkc):
                    rel = kc_ - 4 * qt
                    attend(kw, kwb, half, kc_ * 128, vw[:, kc_, :], vwb, wm[:, rel + 4, :], wmb, q_ap, qb_, po, pob,
                           kc_ == lo, kc_ == nkc - 1, kc_)
                finish_branch(po, pob, hh * 3 + 2, False, glt, oaccs[r][0], oaccs[r][1])
            p_flush()
            for sub in range(4):
                pt, ptb = kb.psum()
                kb.tr(pt[0:64, 0:128], sel4[:, sub, :], ident[:], [sel4b, identb], [ptb])
                kb.cp("act", selT[:, sub * 128:(sub + 1) * 128], pt[0:64, 0:128], [ptb], [selTb])
            for kc_ in range(nkc):
                pm, pmb = kb.psum()
                kb.mm(pm[:], eall[:, kc_ * 128:(kc_ + 1) * 128], selT[:], True, True, [eallb, selTb], [pmb])
                rel = kc_ - 4 * qt
                if rel >= 0:
                    kb.tt("dve", selm[:, kc_, :], pm[:], wm[:, rel + 4, :], ALU.mult, [pmb, wmb], [selmb])
                else:
                    kb.cp("act", selm[:, kc_, :], pm[:], [pmb], [selmb])
            for r in range(4):
                half = r % 2
                hh = g * 4 + r
                q_ap = q_[half * 64:(half + 1) * 64, r // 2, :]
                oacc, oaccb = oaccs[r]
                po, pob = next_po()
                for kc_ in range(nkc):
                    attend(ks, ksb, half, kc_ * 128, vs[:, kc_, :], vsb, selm[:, kc_, :], selmb, q_ap, qb_, po, pob,
                           kc_ == 0, kc_ == nkc - 1, kc_)
                finish_branch(po, pob, hh * 3 + 1, False, glt, oacc, oaccb)

                def out_fn(r=r, half=half, oacc=oacc, oaccb=oaccb, g=g, t0=t0):
                    ob_, obb_ = obuf[(r // 2) % 2]
                    kb.cp("act", ob_[half * 64:(half + 1) * 64, :], oacc[:], [oaccb], [obb_])
                    if half == 1:
                        cch = g * 2 + r // 2
                        kb.dma(sc["ocT"].ap()[cch * 128:(cch + 1) * 128, t0:t0 + 512], ob_[:], obb_, [obb_], ())
                p_add(LA + 3, out_fn)
                p_tick()
        p_flush()
    P.barrier()
    A.release(mk)
    kb.nrot = 8


def phase_merge(E, l, dst):
    kb, A, P = E["kb"], E["kb"].A, E["kb"].P
    d = E["lay"][l]
    sc = E["scr"]
    ones, onesb = E["ones"]
    mo, mob = E["mods"][l]
    m1, m1b = E["mod1"][l]
    (lg, lgb), (lb_, lbb) = E["lnp"][l]
    hT = E["hT"]
    mk = A.mark()
    kb.nrot = 8

    def ld(name, shape, src):
        t, b = kb.sb(shape, F32, name, dma=True)
        kb.dma(t[:], src.ap(), b, (), [b])
        return t, b
    nag, nagb = ld("nag", [128, 8], d["normag"])
    nab, nabb = ld("nab", [128, 8], d["normab"])
    snw, snwb = ld("snw", [128, 8], d["ssmnw"])
    W = {}
    for nm, nch in (("waout", 8), ("wbout", 8), ("wcout", 8), ("wo", 8), ("wgate", 24)):
        W[nm] = kb.sb([128, nch, 1024], BF16, nm)
    mk2 = A.mark()
    stg = [kb.sb([128, 1024], F32, "stg", dma=True) for _ in range(3)]
    i = 0
    for nm, nch in (("waout", 8), ("wbout", 8), ("wcout", 8), ("wo", 8), ("wgate", 24)):
        t, b = W[nm]
        for c in range(nch):
            st, stb = stg[i % 3]
            kb.dma(st[:], d[nm].ap()[c], stb, (), [stb])
            kb.cp("pool" if i % 2 == 0 else "dve", t[:, c, :], st[:], [stb], [b])
            i += 1
    P.barrier()
    A.release(mk2)
    h, hb = kb.sb([128, 8, 512], F32, "h", dma=True)
    u, ub = kb.sb([128, 8, 512], BF16, "u")
    raw, rawb = kb.sb([128, 8, 512], F32, "raw", dma=True)
    aA, aAb = kb.sb([128, 8, 512], BF16, "aA")
    aB, aBb = kb.sb([128, 8, 512], BF16, "aB")
    aC, aCb = kb.sb([128, 8, 512], BF16, "aC", dma=True)
    mg, mgb = kb.sb([128, 8, 512], BF16, "mg")
    sig = [kb.sb([128, 512], F32, "sig") for _ in range(2)]
    msum, msumb = kb.sb([128, 512], F32, "msum")
    mt, mtb = kb.sb([128, 512], F32, "mt")
    Cn = {"mean": kb.sb([128, 512], F32, "mean"), "rstd": kb.sb([128, 512], F32, "rstd"),
          "sq": kb.sb([128, 512], F32, "sq"), "epscol": E["epsc"]}
    mean, meanb = Cn["mean"]
    rstd, rstdb = Cn["rstd"]
    si = 0
    for t in range(8):
        tsl = slice(t * 512, (t + 1) * 512)
        for k in range(8):
            kb.dma(h[:, k, :], hT.ap()[k * 128:(k + 1) * 128, tsl], hb, (), [hb])
        for k in range(8):
            kb.act(u[:, k, :], h[:, k, :], AF.Identity, [hb, mob, m1b], [ub],
                   bias=mo[:, 24 + k:24 + k + 1], scale=m1[:, 32 + k:32 + k + 1])
        for k in range(8):
            kb.dma(raw[:, k, :], sc["aconvT"].ap()[k * 128:(k + 1) * 128, tsl], rawb, (), [rawb])
        colnorm_stats(kb, Cn, lambda k: (raw[:, k, :], rawb), 8, 512, ones, onesb, Cn["sq"][0], Cn["sq"][1], LN_EPS)
        for k in range(8):
            eng = "dve" if k % 2 == 0 else "pool"
            kb.tt(eng, raw[:, k, :], raw[:, k, :], mean[:], ALU.subtract, [rawb, meanb], [rawb])
            kb.tt(eng, raw[:, k, :], raw[:, k, :], rstd[:], ALU.mult, [rawb, rstdb], [rawb])
            kb.act(raw[:, k, :], raw[:, k, :], AF.Identity, [rawb, nagb, nabb], [rawb], bias=nab[:, k:k + 1], scale=nag[:, k:k + 1])
            kb.act(aA[:, k, :], raw[:, k, :], AF.Silu, [rawb], [aAb])
        for k in range(8):
            kb.dma(raw[:, k, :], sc["ybT"].ap()[k * 128:(k + 1) * 128, tsl], rawb, (), [rawb])
        colnorm_stats(kb, Cn, lambda k: (raw[:, k, :], rawb), 8, 512, ones, onesb, Cn["sq"][0], Cn["sq"][1], LN_EPS, center=False)
        for k in range(8):
            eng = "dve" if k % 2 == 0 else "pool"
            kb.tt(eng, raw[:, k, :], raw[:, k, :], rstd[:], ALU.mult, [rawb, rstdb], [rawb])
            kb.act(aB[:, k, :], raw[:, k, :], AF.Identity, [rawb, snwb], [aBb], scale=snw[:, k:k + 1])
        for k in range(8):
            kb.dma(aC[:, k, :], sc["ocT"].ap()[k * 128:(k + 1) * 128, tsl], aCb, (), [aCb])
        for c in range(8):
            for bi, (wn, act_, actb) in enumerate((("waout", aA, aAb), ("wbout", aB, aBb), ("wcout", aC, aCb))):
                wt, wtb = W[wn]
                wg_, wgb_ = W["wgate"]
                py, pyb = kb.psum()
                pg, pgb = kb.psum()
                for k in range(8):
                    kb.mm(py[:], wt[:, c, k * 128:(k + 1) * 128], act_[:, k, :], k == 0, k == 7, [wtb, actb], [pyb])
                for k in range(8):
                    kb.mm(pg[:], wg_[:, bi * 8 + c, k * 128:(k + 1) * 128], u[:, k, :], k == 0, k == 7, [wgb_, ub], [pgb])
                s_, s_b = sig[si % 2]
                si += 1
                kb.act(s_[:], pg[:], AF.Sigmoid, [pgb], [s_b])
                if bi == 0:
                    kb.tt("dve", msum[:], s_[:], py[:], ALU.mult, [s_b, pyb], [msumb])
                elif bi == 1:
                    kb.tt("dve", mt[:], s_[:], py[:], ALU.mult, [s_b, pyb], [mtb])
                    kb.tt("pool", msum[:], msum[:], mt[:], ALU.add, [msumb, mtb], [msumb])
                else:
                    kb.tt("dve", mt[:], s_[:], py[:], ALU.mult, [s_b, pyb], [mtb])
                    kb.tt("pool", mg[:, c, :], msum[:], mt[:], ALU.add, [msumb, mtb], [mgb])
        wo, wob = W["wo"]
        for c in range(8):
            py, pyb = kb.psum()
            for k in range(8):
                kb.mm(py[:], wo[:, c, k * 128:(k + 1) * 128], mg[:, k, :], k == 0, k == 7, [wob, mgb], [pyb])
            s_, s_b = sig[si % 2]
            si += 1
            kb.act(s_[:], py[:], AF.Identity, [pyb, m1b], [s_b], scale=m1[:, 40 + c:40 + c + 1])
            kb.stt(h[:, c, :], h[:, c, :], DN_ALPHA, s_[:], ALU.mult, ALU.add, [hb, s_b], [hb])
        colnorm_stats(kb, Cn, lambda k: (h[:, k, :], hb), 8, 512, ones, onesb, Cn["sq"][0], Cn["sq"][1], LN_EPS)
        for k in range(8):
            eng = "dve" if k % 2 == 0 else "pool"
            kb.tt(eng, h[:, k, :], h[:, k, :], mean[:], ALU.subtract, [hb, meanb], [hb])
            kb.tt(eng, h[:, k, :], h[:, k, :], rstd[:], ALU.mult, [hb, rstdb], [hb])
            kb.act(h[:, k, :], h[:, k, :], AF.Identity, [hb, lgb, lbb], [hb], bias=lb_[:, 8 + k:8 + k + 1], scale=lg[:, 8 + k:8 + k + 1])
        for k in range(8):
            kb.dma(dst.ap()[k * 128:(k + 1) * 128, tsl], h[:, k, :], hb, [hb], ())
    P.barrier()
    A.release(mk)


def build(nlayers=L, stop_after=None, debug=False, mixsub=None):
    nc = bass.Bass("TRN2", target_bir_lowering=False)
    kb = KB(nc)
    P = kb.P
    A = kb.A
    xT = kb.inp("xT", [D, S])
    ccol = kb.inp("ccol", [128, 8])
    outT = nc.dram_tensor("outT", [D, S], F32, kind="ExternalOutput")
    hT = kb.scratch("hT", [D, S])
    hTb = [kb.buf("hT") for _ in range(4)]
    lay = []
    for l in range(nlayers):
        d = {}
        d["adaw"] = kb.inp("adaw%d" % l, [72, 128, 1024])
        d["adab"] = kb.inp("adab%d" % l, [128, 72])
        d["lng"] = kb.inp("lng%d" % l, [128, 24])
        d["lnb"] = kb.inp("lnb%d" % l, [128, 24])
        for i in range(2):
            d["wg%d" % i] = kb.inp("wg%d_%d" % (l, i), [22, 128, 1024])
            d["wu%d" % i] = kb.inp("wu%d_%d" % (l, i), [22, 128, 1024])
            d["wd%d" % i] = kb.inp("wd%d_%d" % (l, i), [8, 128, 22 * 128])
        for nm, shp in MIX_SHAPES.items():
            d[nm] = kb.inp("%s%d" % (nm, l), shp)
        lay.append(d)
    consts = {nm: kb.inp("c_" + nm, shp) for nm, shp in CONST_SHAPES.items()}
    scr = {"aconvT": kb.scratch("aconvT", [D, S]), "zsT": kb.scratch("zsT", [D, S]), "xsT": kb.scratch("xsT", [D, S]),
           "BT": kb.scratch("BT", [512, S]), "CT": kb.scratch("CT", [512, S]), "glT": kb.scratch("glT", [128, S]),
           "qT": kb.scratch("qT", [D, S], BF16), "kcT": kb.scratch("kcT", [256, S], BF16), "vcT": kb.scratch("vcT", [256, S], BF16),
           "ksT": kb.scratch("ksT", [256, S], BF16), "kwT": kb.scratch("kwT", [256, S], BF16),
           "vsw": kb.scratch("vsw", [S, 512], BF16), "ybT": kb.scratch("ybT", [D, S]), "ocT": kb.scratch("ocT", [D, S], BF16)}
    if debug:
        scr["ybT"] = nc.dram_tensor("dbg_ybT", [D, S], F32, kind="ExternalOutput")
        scr["aconvT"] = nc.dram_tensor("dbg_aconvT", [D, S], F32, kind="ExternalOutput")
        scr["ocT"] = nc.dram_tensor("dbg_ocT", [D, S], BF16, kind="ExternalOutput")

    ones, onesb = kb.sb([128, 128], F32, "ones")
    kb.memset("dve", ones[:], 1.0, [onesb])
    epsc, epscb = kb.sb([128, 1], F32, "eps")
    kb.memset("dve", epsc[:], LN_EPS, [epscb])
    cc, ccb = kb.sb([128, 8], F32, "cc", dma=True)
    kb.dma(cc[:], ccol.ap(), ccb, (), [ccb])
    sc, scb = kb.sb([128, 8], F32, "sc")
    kb.act(sc[:], cc[:], AF.Silu, [ccb], [scb])
    mods = [kb.sb([128, 72], F32, "mod") for _ in range(nlayers)]
    mod1 = [kb.sb([128, 72], F32, "mod1") for _ in range(nlayers)]
    lnp = [(kb.sb([128, 24], F32, "lng", dma=True), kb.sb([128, 24], F32, "lnb", dma=True)) for _ in range(nlayers)]
    dt_pers = kb.sb([128, 32, 16], F32, "dtp")
    E = {"kb": kb, "lay": lay, "scr": scr, "consts": consts, "mods": mods, "mod1": mod1, "lnp": lnp, "hT": hT,
         "ones": (ones, onesb), "epsc": (epsc, epscb), "dt": dt_pers}
    base_mark = A.mark()

    def phase_ada(l):
        m = A.mark()
        d = lay[l]
        ab, abb = kb.sb([128, 72], F32, "adab", dma=True)
        kb.dma(ab[:], d["adab"].ap(), abb, (), [abb])
        (g, gb), (b_, bb) = lnp[l]
        kb.dma(g[:], d["lng"].ap(), gb, (), [gb])
        kb.dma(b_[:], d["lnb"].ap(), bb, (), [bb])
        slots = [kb.sb([128, 1024], F32, "adaw", dma=True) for _ in range(3)]
        ps, psb = kb.psum()
        for j in range(72):
            w, wb = slots[j % 3]
            kb.dma(w[:], d["adaw"].ap()[j], wb, (), [wb])
            for k in range(8):
                kb.mm(ps[:, j:j + 1], w[:, k * 128:(k + 1) * 128], sc[:, k:k + 1], k == 0, k == 7, [wb, scb], [psb])
        mo, mob = mods[l]
        kb.tt("dve", mo[:], ps[:, 0:72], ab[:], ALU.add, [psb, abb], [mob])
        m1, m1b = mod1[l]
        kb.cp("dve", m1[:], mo[:], [mob], [m1b])
        for sub in range(3):
            c0 = (sub * 3 + 1) * 8
            kb.ts("dve", m1[:, c0:c0 + 8], mo[:, c0:c0 + 8], 1.0, None, ALU.add, None, [mob], [m1b])
            c0 = (sub * 3 + 2) * 8
            if sub != 1:
                kb.ts("dve", m1[:, c0:c0 + 8], mo[:, c0:c0 + 8], 0.5, None, ALU.mult, None, [mob], [m1b])
        P.barrier()
        A.release(m)

    def ln_apply(l, i, r, rb, hsrc_cols, NTOK, C):
        (g, gb), (b_, bb) = lnp[l]
        for t0 in range(0, NTOK, 512):
            colnorm_stats(kb, C, lambda k: (r[:, k, t0:t0 + 512], rb), 8, 512, ones, onesb, C["sq"][0], C["sq"][1], LN_EPS)
            mean, meanb = C["mean"]
            rstd, rstdb = C["rstd"]
            for k in range(8):
                eng = "dve" if k % 2 == 0 else "pool"
                kb.tt(eng, r[:, k, t0:t0 + 512], r[:, k, t0:t0 + 512], mean[:, 0:512], ALU.subtract, [rb, meanb], [rb])
                kb.tt(eng, r[:, k, t0:t0 + 512], r[:, k, t0:t0 + 512], rstd[:, 0:512], ALU.mult, [rb, rstdb], [rb])
                kb.act(r[:, k, t0:t0 + 512], r[:, k, t0:t0 + 512], AF.Identity, [rb, gb, bb], [rb],
                       bias=b_[:, i * 8 + k:i * 8 + k + 1], scale=g[:, i * 8 + k:i * 8 + k + 1])

    def phase_ffn(l, i, src, dst):
        sub = 0 if i == 0 else 2
        d = lay[l]
        m1, m1b = mod1[l]
        mo, mob = mods[l]
        m = A.mark()
        NT = 1024
        h, hb = kb.sb([128, 8, NT], F32, "h", dma=True)
        u, ub = kb.sb([128, 8, NT], BF16, "u")
        a, ab = kb.sb([128, 22, NT], BF16, "a")
        sg, sgb = kb.sb([128, 512], F32, "sg")
        sg2, sg2b = kb.sb([128, 512], F32, "sg2")
        C = {"mean": kb.sb([128, 512], F32, "mean"), "rstd": kb.sb([128, 512], F32, "rstd"),
             "sq": kb.sb([128, 512], F32, "sq"), "epscol": (epsc, epscb)}
        wgs = WStream(kb, 1024, 2, "wg")
        wus = WStream(kb, 1024, 2, "wu")
        wds = WStream(kb, 22 * 128, 2, "wd")
        for qt in range(S // NT):
            tsl = slice(qt * NT, (qt + 1) * NT)
            for k in range(8):
                kb.dma(h[:, k, :], src.ap()[k * 128:(k + 1) * 128, tsl], hb, [hTb[qt]], [hb])
            wgs.prefetch(d["wg%d" % i].ap()[0])
            wus.prefetch(d["wu%d" % i].ap()[0])
            for k in range(8):
                kb.act(u[:, k, :], h[:, k, :], AF.Identity, [hb, mob, m1b], [ub],
                       bias=mo[:, (sub * 3) * 8 + k:(sub * 3) * 8 + k + 1],
                       scale=m1[:, (sub * 3 + 1) * 8 + k:(sub * 3 + 1) * 8 + k + 1])
            for f in range(22):
                if f + 1 < 22:
                    wgs.prefetch(d["wg%d" % i].ap()[f + 1])
                    wus.prefetch(d["wu%d" % i].ap()[f + 1])
                else:
                    wds.prefetch(d["wd%d" % i].ap()[0])
                wg, wgb = wgs.get("pool")
                wu, wub = wus.get("pool")
                nt_ = NT // 512
                pgs = [kb.psum() for _ in range(nt_)]
                pus = [kb.psum() for _ in range(nt_)]
                for k in range(8):
                    for t in range(nt_):
                        kb.mm(pgs[t][0][:], wg[:, k * 128:(k + 1) * 128], u[:, k, t * 512:(t + 1) * 512], k == 0, k == 7, [wgb, ub], [pgs[t][1]])
                for k in range(8):
                    for t in range(nt_):
                        kb.mm(pus[t][0][:], wu[:, k * 128:(k + 1) * 128], u[:, k, t * 512:(t + 1) * 512], k == 0, k == 7, [wub, ub], [pus[t][1]])
                for t in range(nt_):
                    t0 = t * 512
                    s_, s_b = (sg, sgb) if t % 2 == 0 else (sg2, sg2b)
                    kb.act(s_[:], pgs[t][0][:], AF.Silu, [pgs[t][1]], [s_b])
                    kb.tt("dve", a[:, f, t0:t0 + 512], s_[:], pus[t][0][:], ALU.mult, [s_b, pus[t][1]], [ab])
            for dd in range(8):
                if dd + 1 < 8:
                    wds.prefetch(d["wd%d" % i].ap()[dd + 1])
                wd, wdb = wds.get("pool")
                for t in range(NT // 512):
                    t0 = t * 512
                    py, pyb = kb.psum()
                    for f in range(22):
                        kb.mm(py[:], wd[:, f * 128:(f + 1) * 128], a[:, f, t0:t0 + 512], f == 0, f == 21, [wdb, ab], [pyb])
                    s_, s_b = (sg, sgb) if t % 2 == 0 else (sg2, sg2b)
                    kb.act(s_[:], py[:], AF.Identity, [pyb, m1b], [s_b], scale=m1[:, (sub * 3 + 2) * 8 + dd:(sub * 3 + 2) * 8 + dd + 1])
                    kb.stt(h[:, dd, t0:t0 + 512], h[:, dd, t0:t0 + 512], DN_ALPHA, s_[:], ALU.mult, ALU.add, [hb, s_b], [hb])
            ln_apply(l, i if i == 0 else 2, h, hb, None, NT, C)
            for k in range(8):
                kb.dma(dst.ap()[k * 128:(k + 1) * 128, tsl], h[:, k, :], hb, [hb], [hTb[qt]])
        P.barrier()
        A.release(m)

    phases = []
    for l in range(nlayers):
        phases.append(("ada", l))
        phases.append(("ffn0", l))
        phases.append(("mix", l))
        phases.append(("ffn1", l))
    if stop_after is not None:
        phases = phases[:stop_after]
    cur = xT
    n = len(phases)
    for pi, (ph, l) in enumerate(phases):
        last = pi == n - 1
        if ph == "ada":
            phase_ada(l)
        elif ph == "ffn0":
            phase_ffn(l, 0, cur, outT if last else hT)
            cur = hT
        elif ph == "ffn1":
            phase_ffn(l, 1, cur, outT if last else hT)
            cur = hT
        elif ph == "mix":
            sub = mixsub or ("m1", "ssd", "nsa", "merge")
            if "m1" in sub:
                phase_m1(E, l)
            if "ssd" in sub:
                phase_ssd(E, l)
            if "nsa" in sub:
                phase_nsa(E, l)
            if "merge" in sub:
                phase_merge(E, l, outT if last else hT)
            cur = hT
    P.emit()
    return nc


def chunkW(W):
    Kt, N = W.shape
    return np.ascontiguousarray(W.reshape(Kt // 128, 128, N // 128, 128).transpose(2, 1, 0, 3)).reshape(N // 128, 128, Kt)


def colvec(v):
    return np.ascontiguousarray(v.reshape(-1, 128).T)


def prep_inputs(inputs, b, nlayers=L):
    f = np.float32
    m = {}
    m["xT"] = np.ascontiguousarray(inputs["x"][b].T)
    m["ccol"] = colvec(inputs["c"][b])
    for l in range(nlayers):
        m["adaw%d" % l] = chunkW(inputs["ada_w"][l])
        m["adab%d" % l] = colvec(inputs["ada_b"][l])
        m["lng%d" % l] = colvec(inputs["ln_g"][l].reshape(-1))
        m["lnb%d" % l] = colvec(inputs["ln_b"][l].reshape(-1))
        for i in range(2):
            m["wg%d_%d" % (l, i)] = chunkW(inputs["ffn_w_gate"][l, i])
            m["wu%d_%d" % (l, i)] = chunkW(inputs["ffn_w_up"][l, i])
            m["wd%d_%d" % (l, i)] = chunkW(inputs["ffn_w_down"][l, i])
        prep_mixer_inputs(inputs, l, m)
    for k_, v_ in host_consts().items():
        m["c_" + k_] = v_
    return {k: np.ascontiguousarray(v, dtype=f) for k, v in m.items()}


_NC_CACHE = {}


def kernel(**inputs):
    inputs = {k: np.asarray(v) for k, v in inputs.items()}
    if "nc" not in _NC_CACHE:
        _NC_CACHE["nc"] = build()
    nc = _NC_CACHE["nc"]
    in_maps = [prep_inputs(inputs, b) for b in range(NCORES)]
    res = run_bass_kernel_spmd(nc, in_maps, core_ids=list(range(NCORES)))
    out = np.stack([np.ascontiguousarray(res.results[b]["outT"].T) for b in range(NCORES)], 0)
    return out.astype(np.float32)
```
